# Optimizing a Trainium2 kernel written in Bass

```python
import jax, jax.numpy as jnp
from jax import lax
import numpy as np

D_MODEL = 1024
BATCH = 4
SEQ = 4096
DEPTH = 1

N_MEM = 256
MIX_WIDTH = D_MODEL
FOX_HEADS = 8
FOX_HEAD_DIM = 64
FOX_WIDTH = FOX_HEADS * FOX_HEAD_DIM
GMLP_GROUPS = 8
GMLP_GROUP_DIM = 64
GMLP_WIDTH = GMLP_GROUPS * GMLP_GROUP_DIM
CHUNK = 128
Q_BLOCK = 128
Q_OFF = 0
K_OFF = Q_OFF + FOX_WIDTH
V_OFF = K_OFF + FOX_WIDTH
F_OFF = V_OFF + FOX_WIDTH
UV_OFF = F_OFF + FOX_HEADS
IN_COLS = UV_OFF + 2 * GMLP_WIDTH
CA_HEADS = 4
CA_HEAD_DIM = D_MODEL // CA_HEADS
D_FF = 2816
EPS = 1e-6

kernel_name = "hybrid_fox_gmlp_macaron_memxattn"


def rms_norm(x, g):
    xf = x.astype(jnp.float32)
    y = xf * lax.rsqrt(jnp.mean(xf * xf, axis=-1, keepdims=True) + EPS)
    return (y * g.astype(jnp.float32)).astype(x.dtype)


def swiglu(h, w_in, w_out):
    gu = h @ w_in
    g, u = jnp.split(gu, 2, axis=-1)
    return (jax.nn.silu(g) * u) @ w_out


def fox_attention(q, k, v, log_f):
    S = q.shape[2]
    scale = FOX_HEAD_DIM ** -0.5
    c = jnp.cumsum(log_f, axis=-1)
    neg = jnp.finfo(jnp.float32).min
    outs = []
    for i in range(S // Q_BLOCK):
        q0, q1 = i * Q_BLOCK, (i + 1) * Q_BLOCK
        qb, kb, vb = q[:, :, q0:q1], k[:, :, :q1], v[:, :, :q1]
        s = (jnp.einsum('bhqd,bhkd->bhqk', qb, kb).astype(jnp.float32) * scale
             + c[:, :, q0:q1, None] - c[:, :, None, :q1])
        mask = (q0 + jnp.arange(Q_BLOCK))[:, None] >= jnp.arange(q1)[None, :]
        s = jnp.where(mask, s, neg)
        p = jax.nn.softmax(s, axis=-1)
        outs.append(jnp.einsum('bhqk,bhkd->bhqd', p.astype(vb.dtype), vb))
    return jnp.concatenate(outs, axis=2)


def spatial_gating(u, v, w_s, b_s):
    B, S, G, Dg = v.shape
    vc = v.reshape(B, S // CHUNK, CHUNK, G, Dg)
    tril = jnp.tril(jnp.ones((CHUNK, CHUNK), dtype=bool))
    w = jnp.where(tril[None], w_s, jnp.zeros_like(w_s))
    mixed = jnp.einsum('gts,bcsgd->bctgd', w, vc) + b_s.T[None, None, :, :, None]
    return u * mixed.reshape(B, S, G, Dg)


def mem_cross_attention(h, m, w_cq, w_ckv, g_cq, g_ck, w_co):
    B, S, _ = h.shape
    M = m.shape[1]
    q = (h @ w_cq).reshape(B, S, CA_HEADS, CA_HEAD_DIM)
    kv = m @ w_ckv
    k, v = jnp.split(kv, 2, axis=-1)
    k = k.reshape(B, M, CA_HEADS, CA_HEAD_DIM)
    v = v.reshape(B, M, CA_HEADS, CA_HEAD_DIM)
    q = rms_norm(q, g_cq)
    k = rms_norm(k, g_ck)
    s = jnp.einsum('bqhd,bkhd->bhqk', q, k).astype(jnp.float32) * (CA_HEAD_DIM ** -0.5)
    p = jax.nn.softmax(s, axis=-1)
    o = jnp.einsum('bhqk,bkhd->bqhd', p.astype(v.dtype), v).reshape(B, S, D_MODEL)
    return o @ w_co


def setup_inputs(seed: int = 0) -> dict:
    key = jax.random.key(seed)
    ks = jax.random.split(key, 32)
    L = DEPTH

    def w(k, shape, fan_in):
        return jax.random.normal(k, shape, jnp.float32) * (fan_in ** -0.5)

    def g(k, shape):
        return 1.0 + 0.02 * jax.random.normal(k, shape, jnp.float32)

    b_f = (jnp.linspace(1.0, 6.0, FOX_HEADS, dtype=jnp.float32)[None, :]
           + 0.1 * jax.random.normal(ks[7], (L, FOX_HEADS), jnp.float32))
    return {
        "x": jax.random.normal(ks[0], (BATCH, SEQ, D_MODEL), jnp.float32),
        "mem": jax.random.normal(ks[1], (BATCH, N_MEM, D_MODEL), jnp.float32),
        "g_ffn1": g(ks[2], (L, D_MODEL)),
        "w_ffn1_in": w(ks[3], (L, D_MODEL, 2 * D_FF), D_MODEL),
        "w_ffn1_out": w(ks[4], (L, D_FF, D_MODEL), D_FF),
        "g_mix": g(ks[5], (L, D_MODEL)),
        "w_in": w(ks[6], (L, D_MODEL, IN_COLS), D_MODEL),
        "b_f": b_f,
        "g_q": g(ks[8], (L, FOX_HEAD_DIM)),
        "g_k": g(ks[9], (L, FOX_HEAD_DIM)),
        "g_sgu": g(ks[10], (L, GMLP_WIDTH)),
        "w_s": w(ks[11], (L, GMLP_GROUPS, CHUNK, CHUNK), CHUNK),
        "b_s": g(ks[12], (L, GMLP_GROUPS, CHUNK)),
        "g_fox_o": g(ks[13], (L, FOX_WIDTH)),
        "g_gmlp_o": g(ks[14], (L, GMLP_WIDTH)),
        "w_out": w(ks[15], (L, MIX_WIDTH, D_MODEL), MIX_WIDTH),
        "g_ca": g(ks[16], (L, D_MODEL)),
        "g_mem": g(ks[17], (L, D_MODEL)),
        "w_cq": w(ks[18], (L, D_MODEL, D_MODEL), D_MODEL),
        "w_ckv": w(ks[19], (L, D_MODEL, 2 * D_MODEL), D_MODEL),
        "g_cq": g(ks[20], (L, CA_HEAD_DIM)),
        "g_ck": g(ks[21], (L, CA_HEAD_DIM)),
        "w_co": w(ks[22], (L, D_MODEL, D_MODEL), D_MODEL),
        "g_ffn2": g(ks[23], (L, D_MODEL)),
        "w_ffn2_in": w(ks[24], (L, D_MODEL, 2 * D_FF), D_MODEL),
        "w_ffn2_out": w(ks[25], (L, D_FF, D_MODEL), D_FF),
    }


def reference(x, mem, g_ffn1, w_ffn1_in, w_ffn1_out, g_mix, w_in, b_f, g_q, g_k,
              g_sgu, w_s, b_s, g_fox_o, g_gmlp_o, w_out, g_ca, g_mem, w_cq, w_ckv,
              g_cq, g_ck, w_co, g_ffn2, w_ffn2_in, w_ffn2_out):
    B, S, _ = x.shape
    for l in range(DEPTH):
        x = x + 0.5 * swiglu(rms_norm(x, g_ffn1[l]), w_ffn1_in[l], w_ffn1_out[l])

        h = rms_norm(x, g_mix[l])
        z = h @ w_in[l]
        q = z[..., Q_OFF:K_OFF].reshape(B, S, FOX_HEADS, FOX_HEAD_DIM)
        k = z[..., K_OFF:V_OFF].reshape(B, S, FOX_HEADS, FOX_HEAD_DIM)
        v = z[..., V_OFF:F_OFF].reshape(B, S, FOX_HEADS, FOX_HEAD_DIM)
        f_logit = z[..., F_OFF:UV_OFF]
        uv = z[..., UV_OFF:]

        q = rms_norm(q, g_q[l])
        k = rms_norm(k, g_k[l])
        log_f = jax.nn.log_sigmoid(f_logit.astype(jnp.float32) + b_f[l].astype(jnp.float32))
        attn = fox_attention(q.transpose(0, 2, 1, 3), k.transpose(0, 2, 1, 3),
                             v.transpose(0, 2, 1, 3), log_f.transpose(0, 2, 1))
        attn = attn.transpose(0, 2, 1, 3).reshape(B, S, FOX_WIDTH)

        uv = jax.nn.gelu(uv)
        u, vg = jnp.split(uv, 2, axis=-1)
        vg = rms_norm(vg, g_sgu[l])
        sgu = spatial_gating(u.reshape(B, S, GMLP_GROUPS, GMLP_GROUP_DIM),
                             vg.reshape(B, S, GMLP_GROUPS, GMLP_GROUP_DIM),
                             w_s[l], b_s[l]).reshape(B, S, GMLP_WIDTH)

        y = jnp.concatenate([rms_norm(attn, g_fox_o[l]), rms_norm(sgu, g_gmlp_o[l])], axis=-1)
        x = x + y @ w_out[l]

        x = x + mem_cross_attention(rms_norm(x, g_ca[l]), rms_norm(mem, g_mem[l]),
                                    w_cq[l], w_ckv[l], g_cq[l], g_ck[l], w_co[l])

        x = x + 0.5 * swiglu(rms_norm(x, g_ffn2[l]), w_ffn2_in[l], w_ffn2_out[l])
    return x
```

```python
import numpy as np
from contextlib import ExitStack
import concourse.bass as bass
import concourse.mybir as mybir
from concourse.bass_utils import run_bass_kernel_spmd

F32 = mybir.dt.float32
BF16 = mybir.dt.bfloat16
AF = mybir.ActivationFunctionType
ALU = mybir.AluOpType
AX = mybir.AxisListType

N_DMA_SEMS = 24
OWN_RUNS = {0: [0, 3, 4, 7], 1: [1, 2, 5, 6]}
NEG = -240000.0
EPS = 1e-6
NF = 22


class Res:
    __slots__ = ("name", "w", "r")

    def __init__(self, name):
        self.name = name
        self.w = None
        self.r = []


class Op:
    __slots__ = ("eng", "fn", "deps", "needed", "token", "is_dma", "slot")

    def __init__(self, eng, fn, is_dma):
        self.eng = eng
        self.fn = fn
        self.deps = []
        self.needed = False
        self.token = None
        self.is_dma = is_dma
        self.slot = None


class Sched:
    ENGS = ("pe", "act", "dve", "pool", "sp")

    def __init__(self, nc):
        self.nc = nc
        self.ops = []
        self.q = {e: [] for e in self.ENGS}
        self.n_dma = 0
        self.slot_last = [None] * N_DMA_SEMS
        self.res = {}

    def R(self, *key):
        r = self.res.get(key)
        if r is None:
            r = self.res[key] = Res(key)
        return r

    def add(self, eng, fn, reads=(), writes=(), dma=False):
        op = Op(eng, fn, dma)
        deps = {}
        writes = list(writes) + [r for r in reads if r.name[0] == "ps"]
        reads = [r for r in reads if r.name[0] != "ps"]

        def dep(d, kind):
            if d is None or d is op:
                return
            if not d.is_dma and d.eng == eng:
                if eng == "pe":
                    return
            deps[id(d)] = d

        for r in reads:
            dep(r.w, "raw")
        for w in writes:
            dep(w.w, "waw")
            for rr in w.r:
                dep(rr, "war")
        if dma:
            op.slot = self.n_dma % N_DMA_SEMS
            self.n_dma += 1
            prev = self.slot_last[op.slot]
            if prev is not None:
                deps[id(prev)] = prev
            self.slot_last[op.slot] = op
        for r in reads:
            r.r.append(op)
        for w in writes:
            w.w = op
            w.r = []
        op.deps = list(deps.values())
        for d in op.deps:
            d.needed = True
        self.ops.append(op)
        self.q[eng].append(op)
        return op

    def fence(self, eng, ops):
        op = Op(eng, None, False)
        op.deps = [o for o in ops if o is not None and o.fn is not None]
        for d in op.deps:
            d.needed = True
        self.ops.append(op)
        self.q[eng].append(op)
        return op

    def barrier(self):
        deps = []
        for e in self.ENGS:
            for o in reversed(self.q[e]):
                if o.fn is not None and not o.is_dma:
                    deps.append(o)
                    break
        deps += [o for o in self.slot_last if o is not None]
        for e in self.ENGS:
            self.fence(e, deps)

    def emit(self):
        nc = self.nc
        with ExitStack() as st:
            esem = {e: st.enter_context(nc.semaphore("s_" + e)) for e in self.ENGS}
            dsem = [st.enter_context(nc.semaphore("d%d" % i)) for i in range(N_DMA_SEMS)]
            cnt = {e: 0 for e in self.ENGS}
            dcnt = [0] * N_DMA_SEMS
            for op in self.ops:
                if op.fn is None:
                    continue
                if op.is_dma:
                    dcnt[op.slot] += 16
                    op.token = (dsem[op.slot], dcnt[op.slot])
                elif op.needed:
                    cnt[op.eng] += 1
                    op.token = (esem[op.eng], cnt[op.eng])
            block = st.enter_context(nc.Block())

            def run(ename, e):
                known = {}
                for op in self.q[ename]:
                    for d in op.deps:
                        sem, v = d.token
                        k = id(sem)
                        if known.get(k, 0) < v:
                            e.wait_ge(sem, v)
                            known[k] = v
                    if op.fn is None:
                        continue
                    ins = op.fn(e)
                    if op.is_dma:
                        ins.then_inc(op.token[0], 16)
                    elif op.needed:
                        ins.then_inc(op.token[0], 1)

            @block.tensor
            def _(e):
                run("pe", e)

            @block.scalar
            def _(e):
                run("act", e)

            @block.vector
            def _(e):
                run("dve", e)

            @block.gpsimd
            def _(e):
                run("pool", e)

            @block.sync
            def _(e):
                run("sp", e)


class Rot:
    def __init__(self, items):
        self.items = list(items)
        self.i = 0

    def next(self):
        v = self.items[self.i % len(self.items)]
        self.i += 1
        return v


GC_FFN1, GC_MIX, GC_CA, GC_MEM, GC_FFN2 = 0, 8, 16, 24, 32
GC_FOXO, GC_GMLPO, GC_Q, GC_K, GC_CQ, GC_CK, GC_BF = 40, 44, 48, 49, 50, 52, 54


def build_program(debug=False, stop_after=None):
    nc = bass.Bass("TRN2", target_bir_lowering=False)
    S = Sched(nc)
    R = S.R

    def din(name, shape):
        return nc.dram_tensor(name, shape, F32, kind="ExternalInput").ap()

    xT = din("xT", [1024, 4096]).rearrange("(k p) t -> p k t", p=128)
    memT = din("memT", [1024, 256]).rearrange("(k p) t -> p k t", p=128)
    w1a = din("w1a", [NF, 128, 2048])
    w1b = din("w1b", [8, 128, NF * 128])
    w2a = din("w2a", [NF, 128, 2048])
    w2b = din("w2b", [8, 128, NF * 128])
    wq_d = din("wq", [128, 4096])
    wk_d = din("wk", [128, 4096])
    wv_d = din("wv", [128, 4096])
    wu_d = din("wu", [128, 4096])
    wvg_d = din("wvg", [128, 4096])
    wf_d = din("wf", [128, 64])
    wo_d = din("wo", [128, 8192])
    wcq_d = din("wcq", [128, 8192])
    wck_d = din("wck", [128, 8192])
    wcv_d = din("wcv", [128, 8192])
    wco_d = din("wco", [128, 8192])
    gvec_d = din("gvec", [128, 64])
    gsgu_d = din("gsgu", [128, 512])
    wsT_d = din("wsT", [128, 1024])
    tril_d = din("tril", [128, 128])
    bsb_d = din("bsb", [128, 512])
    cst_d = din("cst", [128, 512])
    flagcol_d = din("flagcol", [128, 4])
    mord_d = din("mord", [64, 64])
    outT = nc.dram_tensor("outT", [1024, 2048], F32, kind="ExternalOutput").ap().rearrange(
        "(k p) t -> p k t", p=128)
    kind_s = "ExternalOutput" if debug else "Internal"
    Qaug = nc.dram_tensor("Qaug", [8, 70, 2048], BF16, kind=kind_s).ap()
    Kaug = nc.dram_tensor("Kaug", [8, 70, 4096], BF16, kind=kind_s).ap()
    Vaug = nc.dram_tensor("Vaug", [32, 128, 520], BF16, kind=kind_s).ap()
    X1 = nc.dram_tensor("X1", [1024, 2048], F32, kind=kind_s).ap().rearrange("(k p) t -> p k t", p=128)
    Yg = nc.dram_tensor("Yg", [512, 2048], BF16, kind=kind_s).ap().rearrange("(k p) t -> p k t", p=128)
    Attn = nc.dram_tensor("Attn", [512, 2048], F32, kind=kind_s).ap()
    AttnV = Attn.rearrange("(k p) t -> p k t", p=128)
    LFd = nc.dram_tensor("LFd", [8, 4096], F32, kind=kind_s).ap()

    with ExitStack() as st:
        def sb(name, shape, dt):
            return st.enter_context(nc.sbuf_tensor("sb_" + name, shape, dt))

        xg = sb("xg", [128, 8, 1024], F32)
        big_bf = sb("big_bf", [128, 31232], BF16)
        hT = big_bf[:, 0:8192].rearrange("p (k t) -> p k t", k=8)
        actT = big_bf[:, 8192:30720].rearrange("p (f t) -> p f t", f=NF)
        wA = sb("wA", [128, 3, 2048], BF16)
        wB = sb("wB", [128, 2, NF * 128], BF16)
        wM = sb("wM", [128, 8192], BF16)
        big_f = sb("big_f", [128, 8192], F32)
        TFt = [sb("tf%d" % i, [128, 512], F32) for i in range(8)]
        TBt = [sb("tbf%d" % i, [128, 512], BF16) for i in range(6)]
        tfrot = Rot(range(8))
        tbrot = Rot(range(6))

        def TF():
            i = tfrot.next()
            return TFt[i], S.R("tf", i)

        def TB():
            i = tbrot.next()
            return TBt[i], S.R("tbf", i)
        vsb = [sb("vsb%d" % i, [128, 8, 65], BF16) for i in range(2)]
        cst = sb("cst", [128, 4, 128], BF16)
        rA = [sb("rA%d" % i, [128, 512], F32) for i in range(2)]
        gv = sb("gv", [128, 64], F32)
        negbf = sb("negbf", [128, 1], F32)
        sm = sb("sm", [128, 8], F32)
        gsgu = sb("gsgu", [128, 512], F32)
        wsT = sb("wsT", [128, 8, 128], BF16)
        bsb = sb("bsb", [128, 512], F32)
        wf = sb("wf", [128, 8, 8], BF16)
        flagcol = sb("flagcol", [128, 4], F32)
        mord = sb("mord", [64, 64], F32)
        KmT = sb("KmT", [128, 8, 256], BF16)
        Vm = sb("Vm", [128, 2, 1024], BF16)
        pbw = [st.enter_context(nc.psum_tensor("pbw%d" % i, [128, 1024], F32)) for i in range(4)]
        pb = [pbw[i // 2][:, (i % 2) * 512:(i % 2 + 1) * 512] for i in range(8)]

        ident = cst[:, 0, :]
        ones_b = cst[:, 1, :]
        bdiag = cst[:, 2, :]
        maskdiag = cst[:, 3, :]
        psA = Rot([0, 1, 2, 3])
        psC = Rot([4, 5])
        psN = Rot([6, 7])
        wMslot = Rot([0, 1])

        def RP(b):
            return R("ps", b)

        def gcol(c):
            return gv[:, c:c + 1]

        def hs(h):
            return slice(h * 512, (h + 1) * 512)

        def load_x(G, fine=False):
            for hh in (0, 1):
                c0 = G * 1024 + hh * 512
                if fine and hh == 0:
                    for k in range(8):
                        S.add("sp" if k % 2 == 0 else "pool", lambda e, c0=c0, hh=hh, k=k: e.dma_start(
                            out=xg[:, k, hs(hh)], in_=xT[:, k, c0:c0 + 512]), writes=[R("xg", k, hh)], dma=True)
                    continue
                S.add("pool", lambda e, c0=c0, hh=hh: e.dma_start(out=xg[:, :, hs(hh)], in_=xT[:, :, c0:c0 + 512]),
                      writes=[R("xg", k, hh) for k in range(8)], dma=True)

        S.add("sp", lambda e: e.dma_start(out=gv[:], in_=gvec_d), writes=[R("gv")], dma=True)
        load_x(0, fine=True)
        for f0 in range(2):
            S.add("pool", lambda e, f0=f0: e.dma_start(out=wA[:, f0, :], in_=w1a[f0]), writes=[R("wA", f0)], dma=True)
        first_ffn = [True]
        def ld(eng, out, in_, wres):
            return S.add(eng, lambda e: e.dma_start(out=out, in_=in_), writes=wres, dma=True)

        ld("pool", cst[:].rearrange("p a b -> p (a b)"), cst_d, [R("cst")])
        ld("sp", gsgu[:], gsgu_d, [R("gsgu")])
        ld("sp", bsb[:], bsb_d, [R("bsb")])
        ld("sp", mord[:], mord_d, [R("mord")])
        ld("pool", wf[:].rearrange("p a b -> p (a b)"), wf_d, [R("wf")])
        ld("sp", flagcol[:], flagcol_d, [R("flagcol")])
        S.add("pool", lambda e: e.memset(sm[:, 7:8], -0.5), writes=[R("sm", 7)])
        S.add("dve", lambda e: e.tensor_scalar(out=negbf[0:8, :], in0=gv[0:8, GC_BF:GC_BF + 1], scalar1=-1.0,
                                              scalar2=None, op0=ALU.mult), reads=[R("gv")], writes=[R("negbf")])
        for i in range(2):
            S.add("dve", lambda e, i=i: e.memset(vsb[i][:], 1.0), writes=[R("vsb", i)])

        def rstd_from(bank, W, D):
            ta, ra = TF()
            S.add("act", lambda e: e.activation(out=ta[:, 0:W], in_=pb[bank][:, 0:W], func=AF.Ln, bias=EPS, scale=1.0 / D),
                  reads=[RP(bank)], writes=[ra])
            S.add("act", lambda e: e.activation(out=ta[:, 0:W], in_=ta[:, 0:W], func=AF.Exp, scale=-0.5),
                  reads=[ra], writes=[ra])
            return ta, ra

        def pipeline(tiles):
            ns_ = max(len(t) for t in tiles)
            for step in range(len(tiles) + ns_ - 1):
                for s_ in reversed(range(ns_)):
                    i = step - s_
                    if 0 <= i < len(tiles) and s_ < len(tiles[i]):
                        tiles[i][s_]()

        def rmsnorm_fm(src, rsrc, dst, rdst, nk, gc0, D, halves, W=512, sq=None, rsq=None):
            for h in halves:
                bank = psN.next()
                for k in range(nk):
                    S.add("act", lambda e, k=k, h=h: e.activation(out=sq(k, h), in_=src(k, h), func=AF.Square),
                          reads=[rsrc(k, h)], writes=[rsq(k, h)])

                def mmf(e, h=h, bank=bank):
                    ins = None
                    for k in range(nk):
                        ins = e.matmul(pb[bank][:, 0:W], lhsT=ones_b, rhs=sq(k, h), start=(k == 0), stop=(k == nk - 1))
                    return ins
                S.add("pe", mmf, reads=[rsq(k, h) for k in range(nk)] + [R("cst")], writes=[RP(bank)])
                trs, rrs = rstd_from(bank, W, D)
                for k in range(nk):
                    S.add("dve", lambda e, k=k, h=h, trs=trs: e.scalar_tensor_tensor(
                        out=dst(k, h), in0=src(k, h), scalar=gcol(gc0 + k), in1=trs[:, 0:W], op0=ALU.mult, op1=ALU.mult),
                        reads=[rsrc(k, h), rrs, R("gv")], writes=[rdst(k, h)])

        def norm_x_to_h(gc0):
            rmsnorm_fm(lambda k, h: xg[:, k, hs(h)], lambda k, h: R("xg", k, h),
                       lambda k, h: hT[:, k, hs(h)], lambda k, h: R("hT", k, h), 8, gc0, 1024.0, (0, 1),
                       sq=lambda k, h: actT[:, k, hs(h)], rsq=lambda k, h: R("act", k, h))

        def ffn(wa_d, wb_d, gc0):
            def load_a(f):
                s = f % 3
                S.add("pool", lambda e: e.dma_start(out=wA[:, s, :], in_=wa_d[f]), writes=[R("wA", s)], dma=True)

            def load_b(d):
                s = d % 2
                S.add("pool", lambda e: e.dma_start(out=wB[:, s, :], in_=wb_d[d]), writes=[R("wB", s)], dma=True)
            if first_ffn[0]:
                first_ffn[0] = False
            else:
                load_a(0)
                load_a(1)
            for h in (0, 1):
                for k in range(8):
                    S.add("dve", lambda e, k=k, h=h: e.tensor_scalar(out=hT[:, k, hs(h)], in0=xg[:, k, hs(h)], scalar1=gcol(gc0 + k),
                                                                    scalar2=None, op0=ALU.mult),
                          reads=[R("xg", k, h), R("gv")], writes=[R("hT", k, h)])
            for h in (0, 1):
                for k in range(8):
                    S.add("act", lambda e, k=k, h=h: e.activation(out=actT[:, k, hs(h)], in_=xg[:, k, hs(h)], func=AF.Square),
                          reads=[R("xg", k, h)], writes=[R("act", k, h)])

            def rstd_for(h):
                bank = psN.next()

                def mmf(e):
                    ins = None
                    for k in range(8):
                        ins = e.matmul(pb[bank][:], lhsT=ones_b, rhs=actT[:, k, hs(h)], start=(k == 0), stop=(k == 7))
                    return ins
                S.add("pe", mmf, reads=[R("act", k, h) for k in range(8)] + [R("cst")], writes=[RP(bank)])
                S.add("act", lambda e: e.activation(out=rA[h][:], in_=pb[bank][:], func=AF.Ln, bias=EPS, scale=1.0 / 1024),
                      reads=[RP(bank)], writes=[R("rA", h)])
                S.add("act", lambda e: e.activation(out=rA[h][:], in_=rA[h][:], func=AF.Exp, scale=-0.5),
                      reads=[R("rA", h)], writes=[R("rA", h)])

            def evac(f, h, bg, bu):
                t1, r1 = TF()
                t2, r2 = TF()
                S.add("dve", lambda e: e.tensor_tensor(out=t1[:], in0=pb[bg][:], in1=rA[h][:], op=ALU.mult),
                      reads=[RP(bg), R("rA", h)], writes=[r1])
                S.add("act", lambda e: e.activation(out=t1[:], in_=t1[:], func=AF.Silu), reads=[r1], writes=[r1])
                S.add("dve", lambda e: e.tensor_tensor(out=t2[:], in0=pb[bu][:], in1=rA[h][:], op=ALU.mult),
                      reads=[RP(bu), R("rA", h)], writes=[r2])
                S.add("dve", lambda e: e.tensor_tensor(out=actT[:, f, hs(h)], in0=t1[:], in1=t2[:], op=ALU.mult),
                      reads=[r1, r2], writes=[R("act", f, h)])

            for f in range(NF):
                if f + 2 < NF:
                    load_a(f + 2)
                elif f + 2 == NF:
                    load_b(0)
                else:
                    load_b(1)
                s = f % 3
                todo = []
                for h in (0, 1):
                    bg = psA.next()
                    bu = psA.next()

                    def mmf(e, s=s, h=h, bg=bg, bu=bu):
                        ins = None
                        for gu, bank in ((0, bg), (1, bu)):
                            for k in range(8):
                                c0 = gu * 1024 + k * 128
                                ins = e.matmul(pb[bank][:], lhsT=wA[:, s, c0:c0 + 128], rhs=hT[:, k, hs(h)],
                                               start=(k == 0), stop=(k == 7))
                        return ins
                    if f == 0:
                        for gu, bank in ((0, bg), (1, bu)):
                            for k in range(8):
                                c0 = gu * 1024 + k * 128
                                S.add("pe", lambda e, s=s, h=h, bank=bank, k=k, c0=c0: e.matmul(
                                    pb[bank][:], lhsT=wA[:, s, c0:c0 + 128], rhs=hT[:, k, hs(h)], start=(k == 0), stop=(k == 7)),
                                    reads=[R("wA", s), R("hT", k, h)], writes=[RP(bank)])
                        rstd_for(h)
                        todo.append((f, h, bg, bu))
                    else:
                        S.add("pe", mmf, reads=[R("wA", s)] + [R("hT", k, h) for k in range(8)],
                              writes=[RP(bg), RP(bu)])
                        evac(f, h, bg, bu)
                for args in todo:
                    evac(*args)
            for d in range(8):
                s = d % 2
                for h in (0, 1):
                    by = psC.next()

                    def mmf(e, s=s, h=h, by=by):
                        ins = None
                        for f in range(NF):
                            ins = e.matmul(pb[by][:], lhsT=wB[:, s, f * 128:(f + 1) * 128], rhs=actT[:, f, hs(h)],
                                           start=(f == 0), stop=(f == NF - 1))
                        return ins
                    S.add("pe", mmf, reads=[R("wB", s)] + [R("act", f, h) for f in range(NF)], writes=[RP(by)])
                    S.add("dve", lambda e, d=d, h=h, by=by: e.scalar_tensor_tensor(
                        out=xg[:, d, hs(h)], in0=pb[by][:], scalar=0.5, in1=xg[:, d, hs(h)], op0=ALU.mult, op1=ALU.add),
                        reads=[RP(by), R("xg", d, h)], writes=[R("xg", d, h)])
                if d + 2 < 8:
                    load_b(d + 2)

        def load_wM(src_d, ncols, slot):
            if ncols == 8192:
                S.add("pool", lambda e: e.dma_start(out=wM[:, 0:4096], in_=src_d[:, 0:4096]),
                      writes=[R("wM", 0)], dma=True)
                S.add("pool", lambda e: e.dma_start(out=wM[:, 4096:8192], in_=src_d[:, 4096:8192]),
                      writes=[R("wM", 1)], dma=True)
                return 0
            S.add("pool", lambda e: e.dma_start(out=wM[:, slot * 4096:(slot + 1) * 4096], in_=src_d),
                  writes=[R("wM", slot)], dma=True)
            return slot * 4096

        def headnorm_stages(w0, m, h, gc, dst_dram, tok0, W=512, per_k=False):
            c = {}

            def st0():
                c["bank"] = psA.next()
                proj_fm(w0, m, h, c["bank"], per_k=per_k)

            def st1():
                c["sq"], c["rsq"] = TB()
                S.add("act", lambda e: e.activation(out=c["sq"][:, 0:W], in_=pb[c["bank"]][:, 0:W], func=AF.Square),
                      reads=[RP(c["bank"])], writes=[c["rsq"]])

            def st2():
                c["bn"] = psN.next()
                S.add("pe", lambda e: e.matmul(pb[c["bn"]][:, 0:W], lhsT=bdiag, rhs=c["sq"][:, 0:W], start=True, stop=True),
                      reads=[c["rsq"], R("cst")], writes=[RP(c["bn"])])

            def st3():
                c["rs"], c["rrs"] = rstd_from(c["bn"], W, 64.0)

            def st4():
                c["o"], c["ro"] = TB()
                S.add("dve", lambda e: e.scalar_tensor_tensor(out=c["o"][:, 0:W], in0=pb[c["bank"]][:, 0:W], scalar=gcol(gc),
                                                             in1=c["rs"][:, 0:W], op0=ALU.mult, op1=ALU.mult),
                      reads=[RP(c["bank"]), c["rrs"], R("gv")], writes=[c["ro"]])
                for hh in range(2):
                    S.add("sp", lambda e, hh=hh: e.dma_start(out=dst_dram[2 * m + hh, 0:64, tok0:tok0 + W],
                                                            in_=c["o"][hh * 64:(hh + 1) * 64, 0:W]),
                          reads=[c["ro"]], writes=[], dma=True)
            return [st0, st1, st2, st3, st4]

        def proj_fm(w0, m, h, bank, per_k=False):
            def mmf(e):
                ins = None
                for k in range(8):
                    c0 = w0 + k * 512 + m * 128
                    ins = e.matmul(pb[bank][:], lhsT=wM[:, c0:c0 + 128], rhs=hT[:, k, hs(h)], start=(k == 0), stop=(k == 7))
                return ins
            if per_k:
                for k in range(8):
                    c0 = w0 + k * 512 + m * 128
                    S.add("pe", lambda e, k=k, c0=c0: e.matmul(pb[bank][:], lhsT=wM[:, c0:c0 + 128], rhs=hT[:, k, hs(h)],
                                                              start=(k == 0), stop=(k == 7)),
                          reads=[R("wM", w0 // 4096), R("hT", k, h)], writes=[RP(bank)])
                return
            S.add("pe", mmf, reads=[R("wM", w0 // 4096)] + [R("hT", k, h) for k in range(8)], writes=[RP(bank)])

        def proj_tm(w0, blk, bank):
            def mmf(e):
                ins = None
                for k in range(8):
                    c0 = w0 + k * 512
                    ins = e.matmul(pb[bank][:], lhsT=hT[:, k, blk * 128:(blk + 1) * 128], rhs=wM[:, c0:c0 + 512],
                                   start=(k == 0), stop=(k == 7))
                return ins
            S.add("pe", mmf, reads=[R("wM", w0 // 4096)] + [R("hT", k, blk // 4) for k in range(8)], writes=[RP(bank)])

        wAf = wA[:].rearrange("p a b -> p (a b)")
        wBf = wB[:].rearrange("p a b -> p (a b)")

        def wres(buf):
            return [R("wM", 0), R("wM", 1)] if buf == "M" else [R("wA", 0), R("wA", 1), R("wB", 0), R("wB", 1)]

        def wsel(buf, k, c0, width):
            if buf == "M":
                return wM[:, k * 1024 + c0:k * 1024 + c0 + width]
            t = wAf if k < 4 else wBf
            return t[:, (k % 4) * 1024 + c0:(k % 4) * 1024 + c0 + width]

        def prefetch_w(w_d, buf):
            if buf == "M":
                load_wM(w_d, 8192, 0)
            else:
                S.add("pool", lambda e: e.dma_start(out=wAf[:, 0:4096], in_=w_d[:, 0:4096]),
                      writes=[R("wA", 0), R("wA", 1)], dma=True)
                S.add("pool", lambda e: e.dma_start(out=wBf[:, 0:4096], in_=w_d[:, 4096:8192]),
                      writes=[R("wB", 0), R("wB", 1)], dma=True)

        vcount = [0]
        t_ones_rows, r_ones_rows = TB()
        S.add("dve", lambda e: e.memset(t_ones_rows[:], 1.0), writes=[r_ones_rows])
        for h in range(8):
            S.add("sp", lambda e, h=h: e.dma_start(out=Kaug[h, 64:67, :].rearrange("a (r t) -> (a r) t", r=8), in_=t_ones_rows[0:24, :]),
                  reads=[r_ones_rows], writes=[], dma=True)
            S.add("sp", lambda e, h=h: e.dma_start(out=Qaug[h, 67:70, :].rearrange("a (r t) -> (a r) t", r=4), in_=t_ones_rows[0:12, :]),
                  reads=[r_ones_rows], writes=[], dma=True)
        for G in range(4):
            own = G < 2
            t0 = G * 1024
            ffn(w1a, w1b, GC_FFN1)
            if G == 0:
                ws_st = [TF(), TF()]
                tr_st = TF()
                for i in range(2):
                    ld("sp", ws_st[i][0][:], wsT_d[:, i * 512:(i + 1) * 512], [ws_st[i][1]])
                ld("sp", tr_st[0][:, 0:128], tril_d, [tr_st[1]])
                for g in range(8):
                    S.add("dve", lambda e, g=g: e.tensor_tensor(out=wsT[:, g, :], in0=ws_st[g // 4][0][:, (g % 4) * 128:(g % 4 + 1) * 128],
                                                               in1=tr_st[0][:, 0:128], op=ALU.mult),
                          reads=[ws_st[g // 4][1], tr_st[1]], writes=[R("wsT")])
            if own:
                for hh in (0, 1):
                    S.add("pool", lambda e, c0=t0 + hh * 512, hh=hh: e.dma_start(out=X1[:, :, c0:c0 + 512], in_=xg[:, :, hs(hh)]),
                          reads=[R("xg", k, hh) for k in range(8)], writes=[], dma=True)
            norm_x_to_h(GC_MIX)
            w0 = load_wM(wk_d, 4096, wMslot.next())
            pipeline([headnorm_stages(w0, m, h, GC_K, Kaug, t0 + h * 512, per_k=(m == 0 and h == 0))
                      for m in range(4) for h in (0, 1)])
            w0 = load_wM(wv_d, 4096, wMslot.next())
            if G + 1 < 4:
                load_x(G + 1)
            else:
                prefetch_w(wck_d, "AB")
            for blk in range(8):
                bank = psA.next()
                proj_tm(w0, blk, bank)
                vi = vcount[0] % 2
                vcount[0] += 1
                S.add("act", lambda e, bank=bank, vi=vi: e.activation(
                    out=vsb[vi][:, :, 0:64], in_=pb[bank][:].rearrange("p (h d) -> p h d", h=8), func=AF.Copy),
                    reads=[RP(bank)], writes=[R("vsb", vi)])
                S.add("sp", lambda e, vi=vi, gb=G * 8 + blk: e.dma_start(
                    out=Vaug[gb], in_=vsb[vi][:].rearrange("p h d -> p (h d)")),
                    reads=[R("vsb", vi)], writes=[], dma=True)
            for h in (0, 1):
                bank = psN.next()

                def mmf(e, h=h, bank=bank):
                    ins = None
                    for k in range(8):
                        ins = e.matmul(pb[bank][0:8, :], lhsT=wf[:, k, :], rhs=hT[:, k, hs(h)], start=(k == 0), stop=(k == 7))
                    return ins
                S.add("pe", mmf, reads=[R("wf")] + [R("hT", k, h) for k in range(8)], writes=[RP(bank)])
                t1, r1 = TF()
                t2, r2 = TF()
                S.add("act", lambda e, bank=bank, t1=t1: e.activation(out=t1[0:8, :], in_=pb[bank][0:8, :], func=AF.Exp,
                                                                     bias=negbf[0:8, :], scale=-1.0),
                      reads=[RP(bank), R("negbf")], writes=[r1])
                S.add("act", lambda e, t1=t1, t2=t2: e.activation(out=t2[0:8, :], in_=t1[0:8, :], func=AF.Ln, bias=1.0, scale=1.0),
                      reads=[r1], writes=[r2])
                S.add("sp", lambda e, c0=t0 + h * 512, t2=t2: e.dma_start(out=LFd[:, c0:c0 + 512], in_=t2[0:8, :]),
                      reads=[r2], writes=[], dma=True)
            if own:
                w0 = load_wM(wq_d, 4096, wMslot.next())
                pipeline([headnorm_stages(w0, m, h, GC_Q, Qaug, t0 + h * 512) for m in range(4) for h in (0, 1)])
                uT = big_f[:, 0:4096].rearrange("p (m t) -> p m t", m=4)
                sguT = big_f[:, 4096:8192].rearrange("p (m t) -> p m t", m=4)
                w0 = load_wM(wu_d, 4096, wMslot.next())
                for m in range(4):
                    for h in (0, 1):
                        bank = psA.next()
                        proj_fm(w0, m, h, bank)
                        S.add("act", lambda e, bank=bank, m=m, h=h: e.activation(
                            out=uT[:, m, hs(h)], in_=pb[bank][:], func=AF.Gelu_apprx_tanh),
                            reads=[RP(bank)], writes=[R("uT", m, h)])
                w0 = load_wM(wvg_d, 4096, wMslot.next())
                def vg_stages(blk):
                    c = {}
                    sc = 3 * (blk % 2)

                    def st0():
                        c["bank"] = psA.next()
                        proj_tm(w0, blk, c["bank"])
                        c["g"], c["rg"] = TF()
                        S.add("act", lambda e: e.activation(out=c["g"][:], in_=pb[c["bank"]][:], func=AF.Gelu_apprx_tanh),
                              reads=[RP(c["bank"])], writes=[c["rg"]])

                    def st1():
                        c["j"], c["rj"] = TF()
                        c["v"], c["rv"] = TB()
                        S.add("act", lambda e: e.activation(out=c["j"][:], in_=c["g"][:], func=AF.Square, accum_out=sm[:, sc:sc + 1]),
                              reads=[c["rg"]], writes=[c["rj"], R("sm", sc)])
                        S.add("pool", lambda e: e.tensor_scalar(out=sm[:, sc + 1:sc + 2], in0=sm[:, sc:sc + 1], scalar1=1.0 / 512,
                                                               scalar2=EPS, op0=ALU.mult, op1=ALU.add),
                              reads=[R("sm", sc)], writes=[R("sm", sc + 1)])
                        S.add("pool", lambda e: e.tensor_tensor(out=sm[:, sc + 2:sc + 3], in0=sm[:, sc + 1:sc + 2], in1=sm[:, 7:8],
                                                               op=ALU.pow),
                              reads=[R("sm", sc + 1), R("sm", 7)], writes=[R("sm", sc + 2)])
                        S.add("dve", lambda e: e.scalar_tensor_tensor(out=c["v"][:], in0=c["g"][:], scalar=sm[:, sc + 2:sc + 3],
                                                                     in1=gsgu[:], op0=ALU.mult, op1=ALU.mult),
                              reads=[c["rg"], R("sm", sc + 2), R("gsgu")], writes=[c["rv"]])

                    def st2():
                        c["bm"] = psA.next()

                        def mmf(e):
                            ins = None
                            for g in range(8):
                                m, hh = g // 2, g % 2
                                ins = e.matmul(pb[c["bm"]][hh * 64:(hh + 1) * 64, m * 128:(m + 1) * 128],
                                               lhsT=c["v"][:, g * 64:(g + 1) * 64], rhs=wsT[:, g, :], start=True, stop=True)
                            return ins
                        S.add("pe", mmf, reads=[c["rv"], R("wsT")], writes=[RP(c["bm"])])

                    def st3():
                        c["m"], c["rm"] = TF()
                        S.add("dve", lambda e: e.tensor_tensor(out=c["m"][:], in0=pb[c["bm"]][:], in1=bsb[:], op=ALU.add),
                              reads=[RP(c["bm"]), R("bsb")], writes=[c["rm"]])
                        S.add("dve", lambda e: e.tensor_tensor(
                            out=sguT[:, :, blk * 128:(blk + 1) * 128], in0=uT[:, :, blk * 128:(blk + 1) * 128],
                            in1=c["m"][:].rearrange("p (m t) -> p m t", m=4), op=ALU.mult),
                            reads=[c["rm"]] + [R("uT", m, blk // 4) for m in range(4)],
                            writes=[R("sguT", blk // 4)])
                    return [st0, st1, st2, st3]

                pipeline([vg_stages(blk) for blk in range(8)])
                rmsnorm_fm(lambda k, h: sguT[:, k, hs(h)], lambda k, h: R("sguT", h),
                           lambda k, h: hT[:, k, hs(h)], lambda k, h: R("hT", k, h), 4, GC_GMLPO, 512.0, (0, 1),
                           sq=lambda k, h: actT[:, k, hs(h)], rsq=lambda k, h: R("act", k, h))
                for k in range(4):
                    S.add("sp", lambda e, k=k, t0=t0: e.dma_start(out=Yg[:, k, t0:t0 + 1024], in_=hT[:, k, :]),
                          reads=[R("hT", k, 0), R("hT", k, 1)], writes=[], dma=True)
        S.barrier()
        if stop_after == "A":
            return finish(nc, S, st)

        mem_f = big_f[:, 0:2048].rearrange("p (k t) -> p k t", k=8)
        load_wM(wcv_d, 8192, 0)
        a2_l, a2_rl = TF()
        a2_c, a2_rc = TF()
        a2_x, a2_rx = TF()
        S.add("sp", lambda e: e.dma_start(out=a2_l[0:64, :], in_=LFd.rearrange("h (r t) -> r h t", r=8)),
              reads=[R("dram_lf")], writes=[a2_rl], dma=True)
        S.add("sp", lambda e: e.dma_start(out=mem_f, in_=memT), writes=[R("memf")], dma=True)
        a2_one, a2_rone = TF()
        S.add("dve", lambda e: e.memset(a2_one[0:64, :], 1.0), writes=[a2_rone])
        S.add("dve", lambda e: e.tensor_tensor_scan(out=a2_c[0:64, :], data0=a2_one[0:64, :], data1=a2_l[0:64, :], initial=0.0,
                                                   op0=ALU.mult, op1=ALU.add),
              reads=[a2_rl, a2_rone], writes=[a2_rc])
        a2_bpre = psN.next()
        S.add("pe", lambda e: e.matmul(pb[a2_bpre][0:64, 0:1], lhsT=mord[:, :], rhs=a2_c[0:64, 511:512], start=True, stop=True),
              reads=[a2_rc, R("mord")], writes=[RP(a2_bpre)])
        S.add("dve", lambda e: e.tensor_copy(out=sm[0:64, 6:7], in_=pb[a2_bpre][0:64, 0:1]), reads=[RP(a2_bpre)], writes=[R("sm", 6)])
        S.add("dve", lambda e: e.tensor_scalar(out=a2_c[0:64, :], in0=a2_c[0:64, :], scalar1=sm[0:64, 6:7], scalar2=8.0,
                                              op0=ALU.add, op1=ALU.mult),
              reads=[a2_rc, R("sm", 6)], writes=[a2_rc])
        spl = []
        for i in range(3):
            t_s, r_s = TB()
            t_n, r_n = TB()
            spl.append((t_s, r_s, t_n, r_n))
            S.add("dve", lambda e, t_s=t_s: e.tensor_copy(out=t_s[0:64, :], in_=a2_c[0:64, :]), reads=[a2_rc], writes=[r_s])
            if i < 2:
                S.add("dve", lambda e, t_s=t_s: e.tensor_copy(out=a2_x[0:64, :], in_=t_s[0:64, :]), reads=[r_s], writes=[a2_rx])
                S.add("dve", lambda e: e.tensor_tensor(out=a2_c[0:64, :], in0=a2_c[0:64, :], in1=a2_x[0:64, :], op=ALU.subtract),
                      reads=[a2_rx, a2_rc], writes=[a2_rc])
            S.add("dve", lambda e, t_s=t_s, t_n=t_n: e.tensor_scalar(out=t_n[0:64, :], in0=t_s[0:64, :], scalar1=-1.0, scalar2=None,
                                                                   op0=ALU.mult), reads=[r_s], writes=[r_n])
            S.add("sp", lambda e, i=i, t_s=t_s: e.dma_start(
                out=Kaug[:, 67 + i, :].rearrange("h (r t) -> r h t", r=8), in_=t_s[0:64, :]),
                reads=[r_s], writes=[], dma=True)
            S.add("sp", lambda e, i=i, t_n=t_n: e.dma_start(
                out=Qaug[:, 64 + i, :].rearrange("h (r t) -> r h t", r=4), in_=t_n[0:32, :]),
                reads=[r_n], writes=[], dma=True)
        vaug = big_bf[:, 0:16640].rearrange("p (b c) -> p b c", b=32)
        for b4 in range(4):
            S.add("sp", lambda e, b4=b4: e.dma_start(out=vaug[:, b4 * 8:(b4 + 1) * 8, :],
                                                    in_=Vaug[b4 * 8:(b4 + 1) * 8].rearrange("b p c -> p b c")),
                  writes=[R("vaug")], dma=True)
        mh = big_bf[:, 16640:18688].rearrange("p (k t) -> p k t", k=8)
        msq = big_bf[:, 18688:20736].rearrange("p (k t) -> p k t", k=8)
        rmsnorm_fm(lambda k, h: mem_f[:, k, :], lambda k, h: R("memf"),
                   lambda k, h: mh[:, k, :], lambda k, h: R("mh", k), 8, GC_MEM, 1024.0, (0,), W=256,
                   sq=lambda k, h: msq[:, k, :], rsq=lambda k, h: R("msq", k))
        for hh in range(4):
            banks = [psA.next(), psA.next()]
            tsq = [TB(), TB()]
            tcp = [TF(), TF()]
            for cc in range(2):
                c16 = hh * 2 + cc

                def mmf(e, c16=c16, bank=banks[cc]):
                    ins = None
                    for k in range(8):
                        ins = e.matmul(pb[bank][:, 0:256], lhsT=wsel("AB", k, c16 * 128, 128), rhs=mh[:, k, :],
                                       start=(k == 0), stop=(k == 7))
                    return ins
                S.add("pe", mmf, reads=wres("AB") + [R("mh", k) for k in range(8)], writes=[RP(banks[cc])])
                S.add("act", lambda e, cc=cc, bank=banks[cc], t=tsq[cc][0]: e.activation(out=t[:, 0:256], in_=pb[bank][:, 0:256],
                                                                                        func=AF.Square),
                      reads=[RP(banks[cc])], writes=[tsq[cc][1]])
                S.add("dve", lambda e, cc=cc, bank=banks[cc], t=tcp[cc][0]: e.tensor_copy(out=t[:, 0:256], in_=pb[bank][:, 0:256]),
                      reads=[RP(banks[cc])], writes=[tcp[cc][1]])
            bn = psN.next()

            def mmn(e, bn=bn, a=tsq[0][0], b=tsq[1][0]):
                e.matmul(pb[bn][:, 0:256], lhsT=ones_b, rhs=a[:, 0:256], start=True, stop=False)
                return e.matmul(pb[bn][:, 0:256], lhsT=ones_b, rhs=b[:, 0:256], start=False, stop=True)
            S.add("pe", mmn, reads=[tsq[0][1], tsq[1][1], R("cst")], writes=[RP(bn)])
            trs, rrs = rstd_from(bn, 256, 256.0)
            for cc in range(2):
                S.add("dve", lambda e, cc=cc, c16=hh * 2 + cc, t=tcp[cc][0], trs=trs: e.scalar_tensor_tensor(
                    out=KmT[:, c16, :], in0=t[:, 0:256], scalar=gcol(GC_CK + cc), in1=trs[:, 0:256],
                    op0=ALU.mult, op1=ALU.mult),
                    reads=[tcp[cc][1], rrs, R("gv")], writes=[R("KmT")])
        for mc in range(2):
            for nh in range(2):
                bank = psA.next()

                def mmf(e, mc=mc, nh=nh, bank=bank):
                    ins = None
                    for k in range(8):
                        ins = e.matmul(pb[bank][:], lhsT=mh[:, k, mc * 128:(mc + 1) * 128], rhs=wsel("M", k, nh * 512, 512),
                                       start=(k == 0), stop=(k == 7))
                    return ins
                S.add("pe", mmf, reads=wres("M") + [R("mh", k) for k in range(8)], writes=[RP(bank)])
                S.add("act", lambda e, mc=mc, nh=nh, bank=bank: e.activation(out=Vm[:, mc, nh * 512:(nh + 1) * 512],
                                                                            in_=pb[bank][:], func=AF.Copy),
                      reads=[RP(bank)], writes=[R("Vm")])
        S.barrier()
        if stop_after == "A2":
            return finish(nc, S, st)

        kaug = [big_bf[:, 16640 + i * 4096:16640 + (i + 1) * 4096] for i in range(2)]
        qaug = [big_bf[:, 24832 + i * 2048:24832 + (i + 1) * 2048] for i in range(2)]
        pT = [big_bf[:, 28928 + i * 1024:28928 + (i + 1) * 1024] for i in range(2)]
        pTrot = Rot([0, 1])
        dtile = Rot([0, 1])
        def load_head(h):
            ks = h % 2
            S.add("sp", lambda e: e.dma_start(out=kaug[ks][0:70, :], in_=Kaug[h]), writes=[R("kaug", ks)], dma=True)
            S.add("sp", lambda e: e.dma_start(out=qaug[ks][0:70, :], in_=Qaug[h]), writes=[R("qaug", ks)], dma=True)

        LOOK = 2
        pending = []

        def finalize(h, r, ba):
            t_n, r_n = TF()
            t_r, r_r = TF()
            t_o, r_o = TF()
            t_h, r_h = TB()
            t_lo, r_lo = TB()

            def st1():
                S.add("dve", lambda e: e.tensor_copy(out=t_n[0:64, :], in_=pb[ba][0:64, :]), reads=[RP(ba)], writes=[r_n])
                S.add("dve", lambda e: e.reciprocal(out=t_r[64:65, :], in_=pb[ba][64:65, :]), reads=[RP(ba)], writes=[r_r])
                S.add("dve", lambda e: e.tensor_copy(out=t_h[64:65, :], in_=t_r[64:65, :]), reads=[r_r], writes=[r_h])
                S.add("dve", lambda e: e.tensor_copy(out=t_o[64:65, :], in_=t_h[64:65, :]), reads=[r_h], writes=[r_o])
                S.add("dve", lambda e: e.tensor_tensor(out=t_r[64:65, :], in0=t_r[64:65, :], in1=t_o[64:65, :], op=ALU.subtract),
                      reads=[r_o, r_r], writes=[r_r])
                S.add("dve", lambda e: e.tensor_copy(out=t_lo[64:65, :], in_=t_r[64:65, :]), reads=[r_r], writes=[r_lo])

            def st2():
                bn = psN.next()

                def mmb(e):
                    e.matmul(pb[bn][0:64, :], lhsT=ones_b[64:65, 0:64], rhs=t_h[64:65, :], start=True, stop=False)
                    return e.matmul(pb[bn][0:64, :], lhsT=ones_b[64:65, 0:64], rhs=t_lo[64:65, :], start=False, stop=True)
                S.add("pe", mmb, reads=[r_h, r_lo, R("cst")], writes=[RP(bn)])
                S.add("dve", lambda e: e.tensor_tensor(out=t_o[0:64, :], in0=pb[bn][0:64, :], in1=t_n[0:64, :], op=ALU.mult),
                      reads=[RP(bn), r_n, r_o], writes=[r_o])
                S.add("sp", lambda e: e.dma_start(out=Attn[h * 64:(h + 1) * 64, r * 512:(r + 1) * 512], in_=t_o[0:64, :]),
                      reads=[r_o], writes=[], dma=True)
            return [st1, st2]

        load_head(0)
        for h in range(8):
            ks = h % 2
            if h + 1 < 8:
                load_head(h + 1)
            for r in range(4):
                ba = psC.next()
                blocks = []
                for i in range(4):
                    blocks.append((16 + 4 * r + i, 0, "flag"))
                for j in range(r):
                    for i in range(4):
                        blocks.append((4 * j + i, 0, None))
                        blocks.append((16 + 4 * j + i, 0, None))
                for i in range(4):
                    blocks.append((4 * r + i, i * 128, "tri"))
                units = []
                i0 = 0
                while i0 < len(blocks):
                    if i0 + 1 < len(blocks) and blocks[i0][1] == 0 and blocks[i0 + 1][1] == 0 \
                            and blocks[i0][2] == blocks[i0 + 1][2] and blocks[i0][2] != "tri":
                        units.append([blocks[i0], blocks[i0 + 1]])
                        i0 += 2
                    else:
                        units.append([blocks[i0]])
                        i0 += 1
                nu = len(units)
                nb = len(blocks)
                info = []
                bcount = [0]
                for idx in range(nu + 1):
                    if idx < nu:
                        unit = units[idx]
                        dt_ = dtile.next()
                        pi = pTrot.next()
                        info.append(pi)

                        def mms(e, unit=unit, dt_=dt_, ks=ks, r=r):
                            ins = None
                            for ui, (kb, c0, mk) in enumerate(unit):
                                N = 512 - c0
                                ins = e.matmul(pbw[dt_][:, ui * 512:ui * 512 + N], lhsT=kaug[ks][0:70, kb * 128:(kb + 1) * 128],
                                               rhs=qaug[ks][0:70, r * 512 + c0:(r + 1) * 512], start=True, stop=(mk != "tri"))
                                if mk == "tri":
                                    ins = e.matmul(pbw[dt_][:, ui * 512:ui * 512 + 128], lhsT=ident, rhs=maskdiag, start=False,
                                                   stop=True, skip_group_check=True)
                            return ins
                        S.add("pe", mms, reads=[R("kaug", ks), R("qaug", ks), R("cst")], writes=[RP(2 * dt_), RP(2 * dt_ + 1)])
                        W_ = 1024 if len(unit) == 2 else 512 - unit[0][1]
                        if unit[0][2] == "flag":
                            S.add("act", lambda e, dt_=dt_, pi=pi, W_=W_, r=r: e.activation(
                                out=pT[pi][:, 0:W_], in_=pbw[dt_][:, 0:W_], func=AF.Exp, scale=0.125, bias=flagcol[:, r:r + 1]),
                                reads=[RP(2 * dt_), RP(2 * dt_ + 1), R("flagcol")], writes=[R("pT", pi)])
                        else:
                            S.add("act", lambda e, dt_=dt_, pi=pi, W_=W_: e.activation(out=pT[pi][:, 0:W_], in_=pbw[dt_][:, 0:W_],
                                                                                     func=AF.Exp, scale=0.125),
                                  reads=[RP(2 * dt_), RP(2 * dt_ + 1)], writes=[R("pT", pi)])
                    if idx >= 1:
                        unit = units[idx - 1]
                        pi = info[idx - 1]
                        first = bcount[0]

                        def mmpv(e, unit=unit, pi=pi, first=first, ba=ba, nb=nb, h=h):
                            ins = None
                            for ui, (kb, c0, mk) in enumerate(unit):
                                N = 512 - c0
                                bi = first + ui
                                ins = e.matmul(pb[ba][0:65, c0:512], lhsT=vaug[:, kb, h * 65:(h + 1) * 65],
                                               rhs=pT[pi][:, ui * 512:ui * 512 + N], start=(bi == 0), stop=(bi == nb - 1),
                                               skip_group_check=True)
                            return ins
                        bcount[0] += len(unit)
                        S.add("pe", mmpv, reads=[R("pT", pi), R("vaug")], writes=[RP(ba)])
                    if idx in (1, min(nu - 1, 5)) and pending:
                        pending.pop(0)()
                while pending:
                    pending.pop(0)()
                pending = finalize(h, r, ba)
        while pending:
            pending.pop(0)()
        load_wM(wo_d, 8192, 0)
        S.barrier()
        if stop_after == "B":
            return finish(nc, S, st)

        S.barrier()
        if stop_after == "M":
            return finish(nc, S, st)

        attnT = big_f[:, 0:4096].rearrange("p (m t) -> p m t", m=4)
        ca_o = actT

        def linear_residual(buf, src, rsrc):
            for d in range(8):
                for h in (0, 1):
                    by = psC.next()

                    def mmf(e, d=d, h=h, by=by):
                        ins = None
                        for k in range(8):
                            ins = e.matmul(pb[by][:], lhsT=wsel(buf, k, d * 128, 128), rhs=src(k, h), start=(k == 0), stop=(k == 7))
                        return ins
                    if d == 0:
                        for k in range(8):
                            S.add("pe", lambda e, d=d, h=h, by=by, k=k: e.matmul(
                                pb[by][:], lhsT=wsel(buf, k, d * 128, 128), rhs=src(k, h), start=(k == 0), stop=(k == 7)),
                                reads=wres(buf) + [rsrc(k, h)], writes=[RP(by)])
                    else:
                        S.add("pe", mmf, reads=wres(buf) + [rsrc(k, h) for k in range(8)], writes=[RP(by)])
                    S.add("dve", lambda e, d=d, h=h, by=by: e.tensor_tensor(out=xg[:, d, hs(h)], in0=pb[by][:],
                                                                           in1=xg[:, d, hs(h)], op=ALU.add),
                          reads=[RP(by), R("xg", d, h)], writes=[R("xg", d, h)])

        out_ops = []

        def dump_x(t0):
            for k in range(8):
                out_ops.append(S.add("sp", lambda e, k=k, t0=t0: e.dma_start(out=outT[:, k, t0:t0 + 1024], in_=xg[:, k, :]),
                                     reads=[R("xg", k, 0), R("xg", k, 1)], writes=[], dma=True))
            S.fence("sp", out_ops)
            return finish(nc, S, st)

        def load_attn(G):
            for hh in (0, 1):
                for k in range(4):
                    S.add("sp" if k % 2 == 0 else "pool", lambda e, k=k, hh=hh, c0=G * 1024 + hh * 512: e.dma_start(
                        out=attnT[:, k, hs(hh)], in_=AttnV[:, k, c0:c0 + 512]), writes=[R("attnT", k, hh)], dma=True)

        def load_yg(G):
            for k in range(4):
                S.add("pool", lambda e, k=k, c0=G * 1024: e.dma_start(out=hT[:, 4 + k, :], in_=Yg[:, k, c0:c0 + 1024]),
                      writes=[R("hT", 4 + k, 0), R("hT", 4 + k, 1)], dma=True)

        for G in range(2):
            t0 = G * 1024
            if G == 0:
                load_attn(G)
                load_yg(G)
            for hh in (0, 1):
                S.add("pool", lambda e, c0=t0 + hh * 512, hh=hh: e.dma_start(out=xg[:, :, hs(hh)], in_=X1[:, :, c0:c0 + 512]),
                      reads=[R("dram_x1")], writes=[R("xg", k, hh) for k in range(8)], dma=True)
            rmsnorm_fm(lambda k, h: attnT[:, k, hs(h)], lambda k, h: R("attnT", k, h),
                       lambda k, h: hT[:, k, hs(h)], lambda k, h: R("hT", k, h), 4, GC_FOXO, 512.0, (0, 1),
                       sq=lambda k, h: actT[:, k, hs(h)], rsq=lambda k, h: R("act", k, h))
            prefetch_w(wcq_d, "AB")
            linear_residual("M", lambda k, h: hT[:, k, hs(h)], lambda k, h: R("hT", k, h))
            if stop_after == "C1":
                return dump_x(t0)
            norm_x_to_h(GC_CA)
            prefetch_w(wco_d, "M")

            def ca_stages(item, hh, h):
                c = {}
                pb0 = (item % 2) * 2

                def st0():
                    c["banks"] = [0, 1]
                    for cc in range(2):
                        c16 = hh * 2 + cc

                        def mmf(e, c16=c16, bank=c["banks"][cc]):
                            ins = None
                            for k in range(8):
                                ins = e.matmul(pb[bank][:], lhsT=wsel("AB", k, c16 * 128, 128), rhs=hT[:, k, hs(h)],
                                               start=(k == 0), stop=(k == 7))
                            return ins
                        if item == 0:
                            for k in range(8):
                                S.add("pe", lambda e, c16=c16, bank=c["banks"][cc], k=k: e.matmul(
                                    pb[bank][:], lhsT=wsel("AB", k, c16 * 128, 128), rhs=hT[:, k, hs(h)],
                                    start=(k == 0), stop=(k == 7)),
                                    reads=wres("AB") + [R("hT", k, h)], writes=[RP(c["banks"][cc])])
                        else:
                            S.add("pe", mmf, reads=wres("AB") + [R("hT", k, h) for k in range(8)], writes=[RP(c["banks"][cc])])

                def st1():
                    c["tsq"] = [TB(), TB()]
                    c["tcp"] = [TF(), TF()]
                    for cc in range(2):
                        S.add("act", lambda e, bank=c["banks"][cc], t=c["tsq"][cc][0]: e.activation(out=t[:], in_=pb[bank][:], func=AF.Square),
                              reads=[RP(c["banks"][cc])], writes=[c["tsq"][cc][1]])
                        S.add("dve", lambda e, bank=c["banks"][cc], t=c["tcp"][cc][0]: e.tensor_copy(out=t[:], in_=pb[bank][:]),
                              reads=[RP(c["banks"][cc])], writes=[c["tcp"][cc][1]])

                def st2():
                    c["bn"] = 6

                    def mmn(e, bn=c["bn"], a=c["tsq"][0][0], b=c["tsq"][1][0]):
                        e.matmul(pb[bn][:], lhsT=ones_b, rhs=a[:], start=True, stop=False)
                        return e.matmul(pb[bn][:], lhsT=ones_b, rhs=b[:], start=False, stop=True)
                    S.add("pe", mmn, reads=[c["tsq"][0][1], c["tsq"][1][1], R("cst")], writes=[RP(c["bn"])])

                def st3():
                    trs, rrs = rstd_from(c["bn"], 512, 256.0)
                    c["tqn"] = [TB(), TB()]
                    for cc in range(2):
                        S.add("dve", lambda e, cc=cc, t=c["tcp"][cc][0], o=c["tqn"][cc][0], trs=trs: e.scalar_tensor_tensor(
                            out=o[:], in0=t[:], scalar=gcol(GC_CQ + cc), in1=trs[:], op0=ALU.mult, op1=ALU.mult),
                            reads=[c["tcp"][cc][1], rrs, R("gv")], writes=[c["tqn"][cc][1]])

                def st4():
                    for mc in range(2):
                        bs_ = 2 + mc

                        def mms(e, mc=mc, bs_=bs_, a=c["tqn"][0][0], b=c["tqn"][1][0]):
                            e.matmul(pb[bs_][:], lhsT=KmT[:, hh * 2, mc * 128:(mc + 1) * 128], rhs=a[:], start=True, stop=False)
                            return e.matmul(pb[bs_][:], lhsT=KmT[:, hh * 2 + 1, mc * 128:(mc + 1) * 128], rhs=b[:],
                                            start=False, stop=True)
                        S.add("pe", mms, reads=[R("KmT"), c["tqn"][0][1], c["tqn"][1][1]], writes=[RP(bs_)])
                        S.add("act", lambda e, mc=mc, bs_=bs_: e.activation(out=actT[:, pb0 + mc, 0:512], in_=pb[bs_][:], func=AF.Exp,
                                                                           scale=1.0 / 16),
                              reads=[RP(bs_)], writes=[R("act", pb0 + mc, 0)])

                def st5():
                    c["bd"] = 7

                    def mmd(e, bn=c["bd"]):
                        e.matmul(pb[bn][:], lhsT=ones_b, rhs=actT[:, pb0, 0:512], start=True, stop=False)
                        return e.matmul(pb[bn][:], lhsT=ones_b, rhs=actT[:, pb0 + 1, 0:512], start=False, stop=True)
                    S.add("pe", mmd, reads=[R("act", pb0, 0), R("act", pb0 + 1, 0), R("cst")], writes=[RP(c["bd"])])
                    c["bo"] = []
                    for cc in range(2):
                        bo = 4 + cc
                        c["bo"].append(bo)
                        c16 = hh * 2 + cc

                        def mmo(e, bo=bo, c16=c16):
                            e.matmul(pb[bo][:], lhsT=Vm[:, 0, c16 * 128:(c16 + 1) * 128], rhs=actT[:, pb0, 0:512], start=True, stop=False)
                            return e.matmul(pb[bo][:], lhsT=Vm[:, 1, c16 * 128:(c16 + 1) * 128], rhs=actT[:, pb0 + 1, 0:512],
                                            start=False, stop=True)
                        S.add("pe", mmo, reads=[R("Vm"), R("act", pb0, 0), R("act", pb0 + 1, 0)], writes=[RP(bo)])

                def st6():
                    t_rd, r_rd = TF()
                    S.add("act", lambda e, bn=c["bd"]: e.activation(out=t_rd[:], in_=pb[bn][:], func=AF.Ln), reads=[RP(c["bd"])], writes=[r_rd])
                    S.add("act", lambda e: e.activation(out=t_rd[:], in_=t_rd[:], func=AF.Exp, scale=-1.0), reads=[r_rd], writes=[r_rd])
                    for cc in range(2):
                        c16 = hh * 2 + cc
                        S.add("dve", lambda e, bo=c["bo"][cc], c16=c16: e.tensor_tensor(
                            out=actT[:, 10 + c16, hs(h)], in0=pb[bo][:], in1=t_rd[:], op=ALU.mult),
                            reads=[RP(c["bo"][cc]), r_rd], writes=[R("act", 10 + c16, h)])
                return [st0, st1, st2, st3, st4, st5, st6]

            pipeline([ca_stages(i, i // 2, i % 2) for i in range(8)])
            if stop_after == "C2a":
                return dump_x(t0)
            linear_residual("M", lambda k, h: actT[:, 10 + k, hs(h)], lambda k, h: R("act", 10 + k, h))
            if stop_after == "C2":
                return dump_x(t0)
            ffn(w2a, w2b, GC_FFN2)
            if G + 1 < 2:
                load_attn(G + 1)
                load_yg(G + 1)
                prefetch_w(wo_d, "M")
            for k in range(8):
                out_ops.append(S.add("sp" if k % 2 == 0 else "pool",
                                     lambda e, k=k, t0=t0: e.dma_start(out=outT[:, k, t0:t0 + 1024], in_=xg[:, k, :]),
                                     reads=[R("xg", k, 0), R("xg", k, 1)], writes=[], dma=True))
        S.fence("sp", out_ops)
        return finish(nc, S, st)


def finish(nc, S, st):
    S.barrier()
    S.emit()
    return nc


def _pk(w, ncols_off, ncols):
    blk = w[:, ncols_off:ncols_off + ncols]
    return np.ascontiguousarray(blk.reshape(8, 128, ncols).transpose(1, 0, 2)).reshape(128, 8 * ncols)


def _ffn_layout(w_in, w_out):
    a = w_in.reshape(8, 128, 2, NF, 128)
    a = np.ascontiguousarray(a.transpose(3, 1, 2, 0, 4)).reshape(NF, 128, 2048)
    b = w_out.reshape(NF, 128, 8, 128)
    b = np.ascontiguousarray(b.transpose(2, 1, 0, 3)).reshape(8, 128, NF * 128)
    return a, b


def _run_order(p):
    own = OWN_RUNS[p]
    oth = OWN_RUNS[1 - p]
    return own + oth


def make_in_maps(inp):
    f32 = lambda a: np.ascontiguousarray(np.asarray(a, dtype=np.float32))
    x = f32(inp["x"])
    mem = f32(inp["mem"])
    w_in = f32(inp["w_in"])[0]
    shared = {}
    shared["w1a"], shared["w1b"] = _ffn_layout(f32(inp["w_ffn1_in"])[0], f32(inp["w_ffn1_out"])[0])
    shared["w2a"], shared["w2b"] = _ffn_layout(f32(inp["w_ffn2_in"])[0], f32(inp["w_ffn2_out"])[0])
    shared["wq"] = _pk(w_in, 0, 512)
    shared["wk"] = _pk(w_in, 512, 512)
    shared["wv"] = _pk(w_in, 1024, 512)
    shared["wf"] = _pk(w_in, 1536, 8)
    shared["wu"] = _pk(w_in, 1544, 512)
    shared["wvg"] = _pk(w_in, 1544 + 512, 512)
    shared["wo"] = _pk(f32(inp["w_out"])[0], 0, 1024)
    shared["wcq"] = _pk(f32(inp["w_cq"])[0], 0, 1024)
    wckv = f32(inp["w_ckv"])[0]
    shared["wck"] = _pk(wckv, 0, 1024)
    shared["wcv"] = _pk(wckv, 1024, 1024)
    shared["wco"] = _pk(f32(inp["w_co"])[0], 0, 1024)
    gvec = np.zeros((128, 64), np.float32)

    def colset(c0, v, n):
        gvec[:, c0:c0 + n] = v.reshape(n, 128).T
    colset(GC_FFN1, f32(inp["g_ffn1"])[0], 8)
    colset(GC_MIX, f32(inp["g_mix"])[0], 8)
    colset(GC_CA, f32(inp["g_ca"])[0], 8)
    colset(GC_MEM, f32(inp["g_mem"])[0], 8)
    colset(GC_FFN2, f32(inp["g_ffn2"])[0], 8)
    colset(GC_FOXO, f32(inp["g_fox_o"])[0], 4)
    colset(GC_GMLPO, f32(inp["g_gmlp_o"])[0], 4)
    gvec[:, GC_Q] = np.tile(f32(inp["g_q"])[0], 2)
    gvec[:, GC_K] = np.tile(f32(inp["g_k"])[0], 2)
    colset(GC_CQ, f32(inp["g_cq"])[0], 2)
    colset(GC_CK, f32(inp["g_ck"])[0], 2)
    gvec[0:8, GC_BF] = f32(inp["b_f"])[0]
    shared["gvec"] = gvec
    shared["gsgu"] = np.ascontiguousarray(np.broadcast_to(f32(inp["g_sgu"])[0][None, :], (128, 512)))
    ws = f32(inp["w_s"])[0]
    shared["wsT"] = np.ascontiguousarray(ws.transpose(2, 0, 1)).reshape(128, 1024)
    s_idx = np.arange(128)
    shared["tril"] = (s_idx[:, None] <= s_idx[None, :]).astype(np.float32)
    bs = f32(inp["b_s"])[0]
    shared["bsb"] = np.ascontiguousarray(bs.reshape(4, 2, 128).transpose(1, 0, 2)[:, None].repeat(64, 1)
                                         .reshape(128, 4, 128)).reshape(128, 512)
    cst = np.zeros((128, 4, 128), np.float32)
    cst[:, 0, :] = np.eye(128, dtype=np.float32)
    cst[:, 1, :] = 1.0
    cst[0:64, 2, 0:64] = 1.0
    cst[64:128, 2, 64:128] = 1.0
    cst[:, 3, :] = np.where(s_idx[:, None] > s_idx[None, :], NEG, 0.0)
    shared["cst"] = cst.reshape(128, 512)
    in_maps = []
    for c in range(8):
        b, p = c // 2, c % 2
        order = _run_order(p)
        xT = x[b].T
        xTp = np.ascontiguousarray(xT.reshape(1024, 8, 512)[:, order, :]).reshape(1024, 4096)
        m = dict(shared)
        m["xT"] = xTp
        m["memT"] = np.ascontiguousarray(mem[b].T)
        mord = np.zeros((8, 8), np.float32)
        for r in range(8):
            for r2 in range(8):
                mord[r, r2] = 1.0 if order[r2] < order[r] else 0.0
        m64 = np.zeros((8, 8, 8, 8), np.float32)
        for hh in range(8):
            m64[:, hh, :, hh] = mord.T
        m["mord"] = m64.reshape(64, 64)
        nf = np.zeros((128, 4), np.float32)
        own, oth = OWN_RUNS[p], OWN_RUNS[1 - p]
        for r in range(4):
            if oth[r] > own[r]:
                nf[:, r] = NEG / 8.0
        m["flagcol"] = nf
        in_maps.append(m)
    return in_maps


_CACHE = {}


def kernel(**inputs):
    if "nc" not in _CACHE:
        _CACHE["nc"] = build_program()
    nc = _CACHE["nc"]
    in_maps = make_in_maps(inputs)
    res = run_bass_kernel_spmd(nc, in_maps, core_ids=list(range(8)))
    out = np.zeros((4, 4096, 1024), np.float32)
    for c in range(8):
        b, p = c // 2, c % 2
        oT = np.asarray(res.results[c]["outT"]).reshape(1024, 4, 512)
        for i, r in enumerate(OWN_RUNS[p]):
            out[b, r * 512:(r + 1) * 512, :] = oT[:, i, :].T
    return out
```

```python
import numpy as np
from contextlib import ExitStack
import concourse.bass as bass
import concourse.mybir as mybir
from concourse.bass_utils import run_bass_kernel_spmd

F32 = mybir.dt.float32
BF16 = mybir.dt.bfloat16
AF = mybir.ActivationFunctionType
ALU = mybir.AluOpType
AX = mybir.AxisListType

N_DMA_SEMS = 32
OWN_RUNS = {0: [0, 3, 4, 7], 1: [1, 2, 5, 6]}
NEG = -240000.0
EPS = 1e-6
NF = 22


class Res:
    __slots__ = ("name", "w", "r")

    def __init__(self, name):
        self.name = name
        self.w = None
        self.r = []


class Op:
    __slots__ = ("eng", "fn", "deps", "needed", "token", "is_dma", "slot")

    def __init__(self, eng, fn, is_dma):
        self.eng = eng
        self.fn = fn
        self.deps = []
        self.needed = False
        self.token = None
        self.is_dma = is_dma
        self.slot = None


class Sched:
    ENGS = ("pe", "act", "dve", "pool", "sp")

    def __init__(self, nc):
        self.nc = nc
        self.ops = []
        self.q = {e: [] for e in self.ENGS}
        self.n_dma = 0
        self.slot_last = [None] * N_DMA_SEMS
        self.res = {}

    def R(self, *key):
        r = self.res.get(key)
        if r is None:
            r = self.res[key] = Res(key)
        return r

    def add(self, eng, fn, reads=(), writes=(), dma=False):
        op = Op(eng, fn, dma)
        deps = {}
        writes = list(writes) + [r for r in reads if r.name[0] == "ps"]
        reads = [r for r in reads if r.name[0] != "ps"]

        def dep(d, kind):
            if d is None or d is op:
                return
            if not d.is_dma and d.eng == eng:
                if eng == "pe":
                    return
            deps[id(d)] = d

        for r in reads:
            dep(r.w, "raw")
        for w in writes:
            dep(w.w, "waw")
            for rr in w.r:
                dep(rr, "war")
        if dma:
            op.slot = self.n_dma % N_DMA_SEMS
            self.n_dma += 1
            prev = self.slot_last[op.slot]
            if prev is not None:
                deps[id(prev)] = prev
            self.slot_last[op.slot] = op
        for r in reads:
            r.r.append(op)
        for w in writes:
            w.w = op
            w.r = []
        op.deps = list(deps.values())
        for d in op.deps:
            d.needed = True
        self.ops.append(op)
        self.q[eng].append(op)
        return op

    def fence(self, eng, ops):
        op = Op(eng, None, False)
        op.deps = [o for o in ops if o is not None and o.fn is not None]
        for d in op.deps:
            d.needed = True
        self.ops.append(op)
        self.q[eng].append(op)
        return op

    def barrier(self):
        deps = []
        for e in self.ENGS:
            for o in reversed(self.q[e]):
                if o.fn is not None and not o.is_dma:
                    deps.append(o)
                    break
        deps += [o for o in self.slot_last if o is not None]
        for e in self.ENGS:
            self.fence(e, deps)

    def emit(self):
        nc = self.nc
        with ExitStack() as st:
            esem = {e: st.enter_context(nc.semaphore("s_" + e)) for e in self.ENGS}
            dsem = [st.enter_context(nc.semaphore("d%d" % i)) for i in range(N_DMA_SEMS)]
            cnt = {e: 0 for e in self.ENGS}
            dcnt = [0] * N_DMA_SEMS
            for op in self.ops:
                if op.fn is None:
                    continue
                if op.is_dma:
                    dcnt[op.slot] += 16
                    op.token = (dsem[op.slot], dcnt[op.slot])
                elif op.needed:
                    cnt[op.eng] += 1
                    op.token = (esem[op.eng], cnt[op.eng])
            block = st.enter_context(nc.Block())

            def run(ename, e):
                known = {}
                for op in self.q[ename]:
                    for d in op.deps:
                        sem, v = d.token
                        k = id(sem)
                        if known.get(k, 0) < v:
                            e.wait_ge(sem, v)
                            known[k] = v
                    if op.fn is None:
                        continue
                    ins = op.fn(e)
                    if op.is_dma:
                        ins.then_inc(op.token[0], 16)
                    elif op.needed:
                        ins.then_inc(op.token[0], 1)

            @block.tensor
            def _(e):
                run("pe", e)

            @block.scalar
            def _(e):
                run("act", e)

            @block.vector
            def _(e):
                run("dve", e)

            @block.gpsimd
            def _(e):
                run("pool", e)

            @block.sync
            def _(e):
                run("sp", e)


class Rot:
    def __init__(self, items):
        self.items = list(items)
        self.i = 0

    def next(self):
        v = self.items[self.i % len(self.items)]
        self.i += 1
        return v


GC_FFN1, GC_MIX, GC_CA, GC_MEM, GC_FFN2 = 0, 8, 16, 24, 32
GC_FOXO, GC_GMLPO, GC_Q, GC_K, GC_CQ, GC_CK, GC_BF = 40, 44, 48, 49, 50, 52, 54


def build_program(debug=False, stop_after=None):
    nc = bass.Bass("TRN2", target_bir_lowering=False)
    S = Sched(nc)
    R = S.R

    def din(name, shape):
        return nc.dram_tensor(name, shape, F32, kind="ExternalInput").ap()

    xT = din("xT", [1024, 4096]).rearrange("(k p) t -> p k t", p=128)
    memT = din("memT", [1024, 256]).rearrange("(k p) t -> p k t", p=128)
    w1a = din("w1a", [NF, 128, 2048])
    w1b = din("w1b", [8, 128, NF * 128])
    w2a = din("w2a", [NF, 128, 2048])
    w2b = din("w2b", [8, 128, NF * 128])
    wq_d = din("wq", [128, 4096])
    wk_d = din("wk", [128, 4096])
    wv_d = din("wv", [128, 4096])
    wu_d = din("wu", [128, 4096])
    wvg_d = din("wvg", [128, 4096])
    wf_d = din("wf", [128, 64])
    wo_d = din("wo", [128, 8192])
    wcq_d = din("wcq", [128, 8192])
    wck_d = din("wck", [128, 8192])
    wcv_d = din("wcv", [128, 8192])
    wco_d = din("wco", [128, 8192])
    gvec_d = din("gvec", [128, 64])
    gsgu_d = din("gsgu", [128, 512])
    wsT_d = din("wsT", [128, 1024])
    tril_d = din("tril", [128, 128])
    bsb_d = din("bsb", [128, 512])
    cst_d = din("cst", [128, 512])
    flagcol_d = din("flagcol", [128, 4])
    mord_d = din("mord", [64, 64])
    outT = nc.dram_tensor("outT", [1024, 2048], F32, kind="ExternalOutput").ap().rearrange(
        "(k p) t -> p k t", p=128)
    kind_s = "ExternalOutput" if debug else "Internal"
    Qaug = nc.dram_tensor("Qaug", [8, 70, 2048], BF16, kind=kind_s).ap()
    Kaug = nc.dram_tensor("Kaug", [8, 70, 4096], BF16, kind=kind_s).ap()
    Vaug = nc.dram_tensor("Vaug", [32, 128, 520], BF16, kind=kind_s).ap()
    X1 = nc.dram_tensor("X1", [1024, 2048], F32, kind=kind_s).ap().rearrange("(k p) t -> p k t", p=128)
    Yg = nc.dram_tensor("Yg", [512, 2048], BF16, kind=kind_s).ap().rearrange("(k p) t -> p k t", p=128)
    Attn = nc.dram_tensor("Attn", [512, 2048], F32, kind=kind_s).ap()
    AttnV = Attn.rearrange("(k p) t -> p k t", p=128)
    LFd = nc.dram_tensor("LFd", [8, 4096], F32, kind=kind_s).ap()

    with ExitStack() as st:
        def sb(name, shape, dt):
            return st.enter_context(nc.sbuf_tensor("sb_" + name, shape, dt))

        xg = sb("xg", [128, 8, 1024], F32)
        big_bf = sb("big_bf", [128, 31232], BF16)
        hT = big_bf[:, 0:8192].rearrange("p (k t) -> p k t", k=8)
        actT = big_bf[:, 8192:30720].rearrange("p (f t) -> p f t", f=NF)
        wA = sb("wA", [128, 3, 2048], BF16)
        wB = sb("wB", [128, 2, NF * 128], BF16)
        wM = sb("wM", [128, 8192], BF16)
        big_f = sb("big_f", [128, 8192], F32)
        TFt = [sb("tf%d" % i, [128, 512], F32) for i in range(8)]
        TBt = [sb("tbf%d" % i, [128, 512], BF16) for i in range(6)]
        tfrot = Rot(range(8))
        tbrot = Rot(range(6))

        def TF():
            i = tfrot.next()
            return TFt[i], S.R("tf", i)

        def TB():
            i = tbrot.next()
            return TBt[i], S.R("tbf", i)
        vsb = [sb("vsb%d" % i, [128, 8, 65], BF16) for i in range(2)]
        cst = sb("cst", [128, 4, 128], BF16)
        rA = [sb("rA%d" % i, [128, 512], F32) for i in range(2)]
        gv = sb("gv", [128, 64], F32)
        negbf = sb("negbf", [128, 1], F32)
        sm = sb("sm", [128, 8], F32)
        gsgu = sb("gsgu", [128, 512], F32)
        wsT = sb("wsT", [128, 8, 128], BF16)
        bsb = sb("bsb", [128, 512], F32)
        wf = sb("wf", [128, 8, 8], BF16)
        flagcol = sb("flagcol", [128, 4], F32)
        mord = sb("mord", [64, 64], F32)
        KmT = sb("KmT", [128, 8, 256], BF16)
        Vm = sb("Vm", [128, 2, 1024], BF16)
        pbw = [st.enter_context(nc.psum_tensor("pbw%d" % i, [128, 1024], F32)) for i in range(4)]
        pb = [pbw[i // 2][:, (i % 2) * 512:(i % 2 + 1) * 512] for i in range(8)]

        ident = cst[:, 0, :]
        ones_b = cst[:, 1, :]
        bdiag = cst[:, 2, :]
        maskdiag = cst[:, 3, :]
        psA = Rot([0, 1, 2, 3])
        psC = Rot([4, 5])
        psN = Rot([6, 7])
        wMslot = Rot([0, 1])

        def RP(b):
            return R("ps", b)

        def gcol(c):
            return gv[:, c:c + 1]

        def hs(h):
            return slice(h * 512, (h + 1) * 512)

        def load_x(G, fine=False):
            for hh in (0, 1):
                c0 = G * 1024 + hh * 512
                if fine and hh == 0:
                    for k in range(8):
                        S.add("sp" if k % 2 == 0 else "pool", lambda e, c0=c0, hh=hh, k=k: e.dma_start(
                            out=xg[:, k, hs(hh)], in_=xT[:, k, c0:c0 + 512]), writes=[R("xg", k, hh)], dma=True)
                    continue
                S.add("pool", lambda e, c0=c0, hh=hh: e.dma_start(out=xg[:, :, hs(hh)], in_=xT[:, :, c0:c0 + 512]),
                      writes=[R("xg", k, hh) for k in range(8)], dma=True)

        S.add("sp", lambda e: e.dma_start(out=gv[:], in_=gvec_d), writes=[R("gv")], dma=True)
        load_x(0, fine=True)
        for f0 in range(2):
            S.add("pool", lambda e, f0=f0: e.dma_start(out=wA[:, f0, :], in_=w1a[f0]), writes=[R("wA", f0)], dma=True)
        first_ffn = [True]
        def ld(eng, out, in_, wres):
            return S.add(eng, lambda e: e.dma_start(out=out, in_=in_), writes=wres, dma=True)

        ld("pool", cst[:].rearrange("p a b -> p (a b)"), cst_d, [R("cst")])
        ld("sp", gsgu[:], gsgu_d, [R("gsgu")])
        ld("sp", bsb[:], bsb_d, [R("bsb")])
        ld("sp", mord[:], mord_d, [R("mord")])
        ld("pool", wf[:].rearrange("p a b -> p (a b)"), wf_d, [R("wf")])
        ld("sp", flagcol[:], flagcol_d, [R("flagcol")])
        S.add("pool", lambda e: e.memset(sm[:, 7:8], -0.5), writes=[R("sm", 7)])
        S.add("dve", lambda e: e.tensor_scalar(out=negbf[0:8, :], in0=gv[0:8, GC_BF:GC_BF + 1], scalar1=-1.0,
                                              scalar2=None, op0=ALU.mult), reads=[R("gv")], writes=[R("negbf")])
        for i in range(2):
            S.add("dve", lambda e, i=i: e.memset(vsb[i][:], 1.0), writes=[R("vsb", i)])

        def rstd_from(bank, W, D):
            ta, ra = TF()
            S.add("act", lambda e: e.activation(out=ta[:, 0:W], in_=pb[bank][:, 0:W], func=AF.Ln, bias=EPS, scale=1.0 / D),
                  reads=[RP(bank)], writes=[ra])
            S.add("act", lambda e: e.activation(out=ta[:, 0:W], in_=ta[:, 0:W], func=AF.Exp, scale=-0.5),
                  reads=[ra], writes=[ra])
            return ta, ra

        def pipeline(tiles):
            ns_ = max(len(t) for t in tiles)
            for step in range(len(tiles) + ns_ - 1):
                for s_ in reversed(range(ns_)):
                    i = step - s_
                    if 0 <= i < len(tiles) and s_ < len(tiles[i]):
                        tiles[i][s_]()

        def rmsnorm_fm(src, rsrc, dst, rdst, nk, gc0, D, halves, W=512, sq=None, rsq=None):
            for h in halves:
                bank = psN.next()
                for k in range(nk):
                    S.add("act", lambda e, k=k, h=h: e.activation(out=sq(k, h), in_=src(k, h), func=AF.Square),
                          reads=[rsrc(k, h)], writes=[rsq(k, h)])

                def mmf(e, h=h, bank=bank):
                    ins = None
                    for k in range(nk):
                        ins = e.matmul(pb[bank][:, 0:W], lhsT=ones_b, rhs=sq(k, h), start=(k == 0), stop=(k == nk - 1))
                    return ins
                S.add("pe", mmf, reads=[rsq(k, h) for k in range(nk)] + [R("cst")], writes=[RP(bank)])
                trs, rrs = rstd_from(bank, W, D)
                for k in range(nk):
                    S.add("dve", lambda e, k=k, h=h, trs=trs: e.scalar_tensor_tensor(
                        out=dst(k, h), in0=src(k, h), scalar=gcol(gc0 + k), in1=trs[:, 0:W], op0=ALU.mult, op1=ALU.mult),
                        reads=[rsrc(k, h), rrs, R("gv")], writes=[rdst(k, h)])

        def norm_x_to_h(gc0):
            rmsnorm_fm(lambda k, h: xg[:, k, hs(h)], lambda k, h: R("xg", k, h),
                       lambda k, h: hT[:, k, hs(h)], lambda k, h: R("hT", k, h), 8, gc0, 1024.0, (0, 1),
                       sq=lambda k, h: actT[:, k, hs(h)], rsq=lambda k, h: R("act", k, h))

        def ffn(wa_d, wb_d, gc0):
            def load_a(f):
                s = f % 3
                S.add("pool", lambda e: e.dma_start(out=wA[:, s, :], in_=wa_d[f]), writes=[R("wA", s)], dma=True)

            def load_b(d):
                s = d % 2
                S.add("pool", lambda e: e.dma_start(out=wB[:, s, :], in_=wb_d[d]), writes=[R("wB", s)], dma=True)
            if first_ffn[0]:
                first_ffn[0] = False
            else:
                load_a(0)
                load_a(1)
            for h in (0, 1):
                for k in range(8):
                    S.add("dve", lambda e, k=k, h=h: e.tensor_scalar(out=hT[:, k, hs(h)], in0=xg[:, k, hs(h)], scalar1=gcol(gc0 + k),
                                                                    scalar2=None, op0=ALU.mult),
                          reads=[R("xg", k, h), R("gv")], writes=[R("hT", k, h)])
            for h in (0, 1):
                for k in range(8):
                    S.add("act", lambda e, k=k, h=h: e.activation(out=actT[:, k, hs(h)], in_=xg[:, k, hs(h)], func=AF.Square),
                          reads=[R("xg", k, h)], writes=[R("act", k, h)])

            def rstd_for(h):
                bank = psN.next()

                def mmf(e):
                    ins = None
                    for k in range(8):
                        ins = e.matmul(pb[bank][:], lhsT=ones_b, rhs=actT[:, k, hs(h)], start=(k == 0), stop=(k == 7))
                    return ins
                S.add("pe", mmf, reads=[R("act", k, h) for k in range(8)] + [R("cst")], writes=[RP(bank)])
                S.add("act", lambda e: e.activation(out=rA[h][:], in_=pb[bank][:], func=AF.Ln, bias=EPS, scale=1.0 / 1024),
                      reads=[RP(bank)], writes=[R("rA", h)])
                S.add("act", lambda e: e.activation(out=rA[h][:], in_=rA[h][:], func=AF.Exp, scale=-0.5),
                      reads=[R("rA", h)], writes=[R("rA", h)])

            def evac(f, h, bg, bu):
                t1, r1 = TF()
                t2, r2 = TF()
                S.add("dve", lambda e: e.tensor_tensor(out=t1[:], in0=pb[bg][:], in1=rA[h][:], op=ALU.mult),
                      reads=[RP(bg), R("rA", h)], writes=[r1])
                S.add("act", lambda e: e.activation(out=t1[:], in_=t1[:], func=AF.Silu), reads=[r1], writes=[r1])
                S.add("dve", lambda e: e.tensor_tensor(out=t2[:], in0=pb[bu][:], in1=rA[h][:], op=ALU.mult),
                      reads=[RP(bu), R("rA", h)], writes=[r2])
                S.add("dve", lambda e: e.tensor_tensor(out=actT[:, f, hs(h)], in0=t1[:], in1=t2[:], op=ALU.mult),
                      reads=[r1, r2], writes=[R("act", f, h)])

            for f in range(NF):
                if f + 2 < NF:
                    load_a(f + 2)
                elif f + 2 == NF:
                    load_b(0)
                else:
                    load_b(1)
                s = f % 3
                todo = []
                for h in (0, 1):
                    bg = psA.next()
                    bu = psA.next()

                    def mmf(e, s=s, h=h, bg=bg, bu=bu):
                        ins = None
                        for gu, bank in ((0, bg), (1, bu)):
                            for k in range(8):
                                c0 = gu * 1024 + k * 128
                                ins = e.matmul(pb[bank][:], lhsT=wA[:, s, c0:c0 + 128], rhs=hT[:, k, hs(h)],
                                               start=(k == 0), stop=(k == 7))
                        return ins
                    if f == 0:
                        for gu, bank in ((0, bg), (1, bu)):
                            for k in range(8):
                                c0 = gu * 1024 + k * 128
                                S.add("pe", lambda e, s=s, h=h, bank=bank, k=k, c0=c0: e.matmul(
                                    pb[bank][:], lhsT=wA[:, s, c0:c0 + 128], rhs=hT[:, k, hs(h)], start=(k == 0), stop=(k == 7)),
                                    reads=[R("wA", s), R("hT", k, h)], writes=[RP(bank)])
                        rstd_for(h)
                        todo.append((f, h, bg, bu))
                    else:
                        S.add("pe", mmf, reads=[R("wA", s)] + [R("hT", k, h) for k in range(8)],
                              writes=[RP(bg), RP(bu)])
                        evac(f, h, bg, bu)
                for args in todo:
                    evac(*args)
            for d in range(8):
                s = d % 2
                for h in (0, 1):
                    by = psC.next()

                    def mmf(e, s=s, h=h, by=by):
                        ins = None
                        for f in range(NF):
                            ins = e.matmul(pb[by][:], lhsT=wB[:, s, f * 128:(f + 1) * 128], rhs=actT[:, f, hs(h)],
                                           start=(f == 0), stop=(f == NF - 1))
                        return ins
                    S.add("pe", mmf, reads=[R("wB", s)] + [R("act", f, h) for f in range(NF)], writes=[RP(by)])
                    S.add("dve", lambda e, d=d, h=h, by=by: e.scalar_tensor_tensor(
                        out=xg[:, d, hs(h)], in0=pb[by][:], scalar=0.5, in1=xg[:, d, hs(h)], op0=ALU.mult, op1=ALU.add),
                        reads=[RP(by), R("xg", d, h)], writes=[R("xg", d, h)])
                if d + 2 < 8:
                    load_b(d + 2)

        def load_wM(src_d, ncols, slot):
            if ncols == 8192:
                S.add("pool", lambda e: e.dma_start(out=wM[:, 0:4096], in_=src_d[:, 0:4096]),
                      writes=[R("wM", 0)], dma=True)
                S.add("pool", lambda e: e.dma_start(out=wM[:, 4096:8192], in_=src_d[:, 4096:8192]),
                      writes=[R("wM", 1)], dma=True)
                return 0
            S.add("pool", lambda e: e.dma_start(out=wM[:, slot * 4096:(slot + 1) * 4096], in_=src_d),
                  writes=[R("wM", slot)], dma=True)
            return slot * 4096

        def headnorm_stages(w0, m, h, gc, dst_dram, tok0, W=512, per_k=False):
            c = {}

            def st0():
                c["bank"] = psA.next()
                proj_fm(w0, m, h, c["bank"], per_k=per_k)

            def st1():
                c["sq"], c["rsq"] = TB()
                S.add("act", lambda e: e.activation(out=c["sq"][:, 0:W], in_=pb[c["bank"]][:, 0:W], func=AF.Square),
                      reads=[RP(c["bank"])], writes=[c["rsq"]])

            def st2():
                c["bn"] = psN.next()
                S.add("pe", lambda e: e.matmul(pb[c["bn"]][:, 0:W], lhsT=bdiag, rhs=c["sq"][:, 0:W], start=True, stop=True),
                      reads=[c["rsq"], R("cst")], writes=[RP(c["bn"])])

            def st3():
                c["rs"], c["rrs"] = rstd_from(c["bn"], W, 64.0)

            def st4():
                c["o"], c["ro"] = TB()
                S.add("dve", lambda e: e.scalar_tensor_tensor(out=c["o"][:, 0:W], in0=pb[c["bank"]][:, 0:W], scalar=gcol(gc),
                                                             in1=c["rs"][:, 0:W], op0=ALU.mult, op1=ALU.mult),
                      reads=[RP(c["bank"]), c["rrs"], R("gv")], writes=[c["ro"]])
                for hh in range(2):
                    S.add("sp", lambda e, hh=hh: e.dma_start(out=dst_dram[2 * m + hh, 0:64, tok0:tok0 + W],
                                                            in_=c["o"][hh * 64:(hh + 1) * 64, 0:W]),
                          reads=[c["ro"]], writes=[], dma=True)
            return [st0, st1, st2, st3, st4]

        def proj_fm(w0, m, h, bank, per_k=False):
            def mmf(e):
                ins = None
                for k in range(8):
                    c0 = w0 + k * 512 + m * 128
                    ins = e.matmul(pb[bank][:], lhsT=wM[:, c0:c0 + 128], rhs=hT[:, k, hs(h)], start=(k == 0), stop=(k == 7))
                return ins
            if per_k:
                for k in range(8):
                    c0 = w0 + k * 512 + m * 128
                    S.add("pe", lambda e, k=k, c0=c0: e.matmul(pb[bank][:], lhsT=wM[:, c0:c0 + 128], rhs=hT[:, k, hs(h)],
                                                              start=(k == 0), stop=(k == 7)),
                          reads=[R("wM", w0 // 4096), R("hT", k, h)], writes=[RP(bank)])
                return
            S.add("pe", mmf, reads=[R("wM", w0 // 4096)] + [R("hT", k, h) for k in range(8)], writes=[RP(bank)])

        def proj_tm(w0, blk, bank):
            def mmf(e):
                ins = None
                for k in range(8):
                    c0 = w0 + k * 512
                    ins = e.matmul(pb[bank][:], lhsT=hT[:, k, blk * 128:(blk + 1) * 128], rhs=wM[:, c0:c0 + 512],
                                   start=(k == 0), stop=(k == 7))
                return ins
            S.add("pe", mmf, reads=[R("wM", w0 // 4096)] + [R("hT", k, blk // 4) for k in range(8)], writes=[RP(bank)])

        wAf = wA[:].rearrange("p a b -> p (a b)")
        wBf = wB[:].rearrange("p a b -> p (a b)")

        def wres(buf):
            return [R("wM", 0), R("wM", 1)] if buf == "M" else [R("wA", 0), R("wA", 1), R("wB", 0), R("wB", 1)]

        def wsel(buf, k, c0, width):
            if buf == "M":
                return wM[:, k * 1024 + c0:k * 1024 + c0 + width]
            t = wAf if k < 4 else wBf
            return t[:, (k % 4) * 1024 + c0:(k % 4) * 1024 + c0 + width]

        def prefetch_w(w_d, buf):
            if buf == "M":
                load_wM(w_d, 8192, 0)
            else:
                S.add("pool", lambda e: e.dma_start(out=wAf[:, 0:4096], in_=w_d[:, 0:4096]),
                      writes=[R("wA", 0), R("wA", 1)], dma=True)
                S.add("pool", lambda e: e.dma_start(out=wBf[:, 0:4096], in_=w_d[:, 4096:8192]),
                      writes=[R("wB", 0), R("wB", 1)], dma=True)

        vcount = [0]
        t_ones_rows, r_ones_rows = TB()
        S.add("dve", lambda e: e.memset(t_ones_rows[:], 1.0), writes=[r_ones_rows])
        for h in range(8):
            S.add("sp", lambda e, h=h: e.dma_start(out=Kaug[h, 64:67, :].rearrange("a (r t) -> (a r) t", r=8), in_=t_ones_rows[0:24, :]),
                  reads=[r_ones_rows], writes=[], dma=True)
            S.add("sp", lambda e, h=h: e.dma_start(out=Qaug[h, 67:70, :].rearrange("a (r t) -> (a r) t", r=4), in_=t_ones_rows[0:12, :]),
                  reads=[r_ones_rows], writes=[], dma=True)
        for G in range(4):
            own = G < 2
            t0 = G * 1024
            ffn(w1a, w1b, GC_FFN1)
            if G == 0:
                ws_st = [TF(), TF()]
                tr_st = TF()
                for i in range(2):
                    ld("sp", ws_st[i][0][:], wsT_d[:, i * 512:(i + 1) * 512], [ws_st[i][1]])
                ld("sp", tr_st[0][:, 0:128], tril_d, [tr_st[1]])
                for g in range(8):
                    S.add("dve", lambda e, g=g: e.tensor_tensor(out=wsT[:, g, :], in0=ws_st[g // 4][0][:, (g % 4) * 128:(g % 4 + 1) * 128],
                                                               in1=tr_st[0][:, 0:128], op=ALU.mult),
                          reads=[ws_st[g // 4][1], tr_st[1]], writes=[R("wsT")])
            if own:
                for hh in (0, 1):
                    S.add("pool", lambda e, c0=t0 + hh * 512, hh=hh: e.dma_start(out=X1[:, :, c0:c0 + 512], in_=xg[:, :, hs(hh)]),
                          reads=[R("xg", k, hh) for k in range(8)], writes=[], dma=True)
            norm_x_to_h(GC_MIX)
            w0 = load_wM(wk_d, 4096, wMslot.next())
            pipeline([headnorm_stages(w0, m, h, GC_K, Kaug, t0 + h * 512, per_k=(m == 0 and h == 0))
                      for m in range(4) for h in (0, 1)])
            w0 = load_wM(wv_d, 4096, wMslot.next())
            if G + 1 < 4:
                load_x(G + 1)
            else:
                prefetch_w(wck_d, "AB")
            for blk in range(8):
                bank = psA.next()
                proj_tm(w0, blk, bank)
                vi = vcount[0] % 2
                vcount[0] += 1
                S.add("act", lambda e, bank=bank, vi=vi: e.activation(
                    out=vsb[vi][:, :, 0:64], in_=pb[bank][:].rearrange("p (h d) -> p h d", h=8), func=AF.Copy),
                    reads=[RP(bank)], writes=[R("vsb", vi)])
                S.add("sp", lambda e, vi=vi, gb=G * 8 + blk: e.dma_start(
                    out=Vaug[gb], in_=vsb[vi][:].rearrange("p h d -> p (h d)")),
                    reads=[R("vsb", vi)], writes=[], dma=True)
            for h in (0, 1):
                bank = psN.next()

                def mmf(e, h=h, bank=bank):
                    ins = None
                    for k in range(8):
                        ins = e.matmul(pb[bank][0:8, :], lhsT=wf[:, k, :], rhs=hT[:, k, hs(h)], start=(k == 0), stop=(k == 7))
                    return ins
                S.add("pe", mmf, reads=[R("wf")] + [R("hT", k, h) for k in range(8)], writes=[RP(bank)])
                t1, r1 = TF()
                t2, r2 = TF()
                S.add("act", lambda e, bank=bank, t1=t1: e.activation(out=t1[0:8, :], in_=pb[bank][0:8, :], func=AF.Exp,
                                                                     bias=negbf[0:8, :], scale=-1.0),
                      reads=[RP(bank), R("negbf")], writes=[r1])
                S.add("act", lambda e, t1=t1, t2=t2: e.activation(out=t2[0:8, :], in_=t1[0:8, :], func=AF.Ln, bias=1.0, scale=1.0),
                      reads=[r1], writes=[r2])
                S.add("sp", lambda e, c0=t0 + h * 512, t2=t2: e.dma_start(out=LFd[:, c0:c0 + 512], in_=t2[0:8, :]),
                      reads=[r2], writes=[], dma=True)
            if own:
                w0 = load_wM(wq_d, 4096, wMslot.next())
                pipeline([headnorm_stages(w0, m, h, GC_Q, Qaug, t0 + h * 512) for m in range(4) for h in (0, 1)])
                uT = big_f[:, 0:4096].rearrange("p (m t) -> p m t", m=4)
                sguT = big_f[:, 4096:8192].rearrange("p (m t) -> p m t", m=4)
                w0 = load_wM(wu_d, 4096, wMslot.next())
                for m in range(4):
                    for h in (0, 1):
                        bank = psA.next()
                        proj_fm(w0, m, h, bank)
                        S.add("act", lambda e, bank=bank, m=m, h=h: e.activation(
                            out=uT[:, m, hs(h)], in_=pb[bank][:], func=AF.Gelu_apprx_tanh),
                            reads=[RP(bank)], writes=[R("uT", m, h)])
                w0 = load_wM(wvg_d, 4096, wMslot.next())
                def vg_stages(blk):
                    c = {}
                    sc = 3 * (blk % 2)

                    def st0():
                        c["bank"] = psA.next()
                        proj_tm(w0, blk, c["bank"])
                        c["g"], c["rg"] = TF()
                        S.add("act", lambda e: e.activation(out=c["g"][:], in_=pb[c["bank"]][:], func=AF.Gelu_apprx_tanh),
                              reads=[RP(c["bank"])], writes=[c["rg"]])

                    def st1():
                        c["j"], c["rj"] = TF()
                        c["v"], c["rv"] = TB()
                        S.add("act", lambda e: e.activation(out=c["j"][:], in_=c["g"][:], func=AF.Square, accum_out=sm[:, sc:sc + 1]),
                              reads=[c["rg"]], writes=[c["rj"], R("sm", sc)])
                        S.add("pool", lambda e: e.tensor_scalar(out=sm[:, sc + 1:sc + 2], in0=sm[:, sc:sc + 1], scalar1=1.0 / 512,
                                                               scalar2=EPS, op0=ALU.mult, op1=ALU.add),
                              reads=[R("sm", sc)], writes=[R("sm", sc + 1)])
                        S.add("pool", lambda e: e.tensor_tensor(out=sm[:, sc + 2:sc + 3], in0=sm[:, sc + 1:sc + 2], in1=sm[:, 7:8],
                                                               op=ALU.pow),
                              reads=[R("sm", sc + 1), R("sm", 7)], writes=[R("sm", sc + 2)])
                        S.add("dve", lambda e: e.scalar_tensor_tensor(out=c["v"][:], in0=c["g"][:], scalar=sm[:, sc + 2:sc + 3],
                                                                     in1=gsgu[:], op0=ALU.mult, op1=ALU.mult),
                              reads=[c["rg"], R("sm", sc + 2), R("gsgu")], writes=[c["rv"]])

                    def st2():
                        c["bm"] = psA.next()

                        def mmf(e):
                            ins = None
                            for g in range(8):
                                m, hh = g // 2, g % 2
                                ins = e.matmul(pb[c["bm"]][hh * 64:(hh + 1) * 64, m * 128:(m + 1) * 128],
                                               lhsT=c["v"][:, g * 64:(g + 1) * 64], rhs=wsT[:, g, :], start=True, stop=True)
                            return ins
                        S.add("pe", mmf, reads=[c["rv"], R("wsT")], writes=[RP(c["bm"])])

                    def st3():
                        c["m"], c["rm"] = TF()
                        S.add("dve", lambda e: e.tensor_tensor(out=c["m"][:], in0=pb[c["bm"]][:], in1=bsb[:], op=ALU.add),
                              reads=[RP(c["bm"]), R("bsb")], writes=[c["rm"]])
                        S.add("dve", lambda e: e.tensor_tensor(
                            out=sguT[:, :, blk * 128:(blk + 1) * 128], in0=uT[:, :, blk * 128:(blk + 1) * 128],
                            in1=c["m"][:].rearrange("p (m t) -> p m t", m=4), op=ALU.mult),
                            reads=[c["rm"]] + [R("uT", m, blk // 4) for m in range(4)],
                            writes=[R("sguT", blk // 4)])
                    return [st0, st1, st2, st3]

                pipeline([vg_stages(blk) for blk in range(8)])
                rmsnorm_fm(lambda k, h: sguT[:, k, hs(h)], lambda k, h: R("sguT", h),
                           lambda k, h: hT[:, k, hs(h)], lambda k, h: R("hT", k, h), 4, GC_GMLPO, 512.0, (0, 1),
                           sq=lambda k, h: actT[:, k, hs(h)], rsq=lambda k, h: R("act", k, h))
                for k in range(4):
                    S.add("sp", lambda e, k=k, t0=t0: e.dma_start(out=Yg[:, k, t0:t0 + 1024], in_=hT[:, k, :]),
                          reads=[R("hT", k, 0), R("hT", k, 1)], writes=[], dma=True)
        S.barrier()
        if stop_after == "A":
            return finish(nc, S, st)

        mem_f = big_f[:, 0:2048].rearrange("p (k t) -> p k t", k=8)
        load_wM(wcv_d, 8192, 0)
        a2_l, a2_rl = TF()
        a2_c, a2_rc = TF()
        a2_x, a2_rx = TF()
        S.add("sp", lambda e: e.dma_start(out=a2_l[0:64, :], in_=LFd.rearrange("h (r t) -> r h t", r=8)),
              reads=[R("dram_lf")], writes=[a2_rl], dma=True)
        S.add("sp", lambda e: e.dma_start(out=mem_f, in_=memT), writes=[R("memf")], dma=True)
        a2_one, a2_rone = TF()
        S.add("dve", lambda e: e.memset(a2_one[0:64, :], 1.0), writes=[a2_rone])
        S.add("dve", lambda e: e.tensor_tensor_scan(out=a2_c[0:64, :], data0=a2_one[0:64, :], data1=a2_l[0:64, :], initial=0.0,
                                                   op0=ALU.mult, op1=ALU.add),
              reads=[a2_rl, a2_rone], writes=[a2_rc])
        a2_bpre = psN.next()
        S.add("pe", lambda e: e.matmul(pb[a2_bpre][0:64, 0:1], lhsT=mord[:, :], rhs=a2_c[0:64, 511:512], start=True, stop=True),
              reads=[a2_rc, R("mord")], writes=[RP(a2_bpre)])
        S.add("dve", lambda e: e.tensor_copy(out=sm[0:64, 6:7], in_=pb[a2_bpre][0:64, 0:1]), reads=[RP(a2_bpre)], writes=[R("sm", 6)])
        S.add("dve", lambda e: e.tensor_scalar(out=a2_c[0:64, :], in0=a2_c[0:64, :], scalar1=sm[0:64, 6:7], scalar2=8.0,
                                              op0=ALU.add, op1=ALU.mult),
              reads=[a2_rc, R("sm", 6)], writes=[a2_rc])
        spl = []
        for i in range(3):
            t_s, r_s = TB()
            t_n, r_n = TB()
            spl.append((t_s, r_s, t_n, r_n))
            S.add("dve", lambda e, t_s=t_s: e.tensor_copy(out=t_s[0:64, :], in_=a2_c[0:64, :]), reads=[a2_rc], writes=[r_s])
            if i < 2:
                S.add("dve", lambda e, t_s=t_s: e.tensor_copy(out=a2_x[0:64, :], in_=t_s[0:64, :]), reads=[r_s], writes=[a2_rx])
                S.add("dve", lambda e: e.tensor_tensor(out=a2_c[0:64, :], in0=a2_c[0:64, :], in1=a2_x[0:64, :], op=ALU.subtract),
                      reads=[a2_rx, a2_rc], writes=[a2_rc])
            S.add("dve", lambda e, t_s=t_s, t_n=t_n: e.tensor_scalar(out=t_n[0:64, :], in0=t_s[0:64, :], scalar1=-1.0, scalar2=None,
                                                                   op0=ALU.mult), reads=[r_s], writes=[r_n])
            S.add("sp", lambda e, i=i, t_s=t_s: e.dma_start(
                out=Kaug[:, 67 + i, :].rearrange("h (r t) -> r h t", r=8), in_=t_s[0:64, :]),
                reads=[r_s], writes=[], dma=True)
            S.add("sp", lambda e, i=i, t_n=t_n: e.dma_start(
                out=Qaug[:, 64 + i, :].rearrange("h (r t) -> r h t", r=4), in_=t_n[0:32, :]),
                reads=[r_n], writes=[], dma=True)
        vaug = big_bf[:, 0:16640].rearrange("p (b c) -> p b c", b=32)
        for b4 in range(4):
            S.add("sp", lambda e, b4=b4: e.dma_start(out=vaug[:, b4 * 8:(b4 + 1) * 8, :],
                                                    in_=Vaug[b4 * 8:(b4 + 1) * 8].rearrange("b p c -> p b c")),
                  writes=[R("vaug")], dma=True)
        mh = big_bf[:, 16640:18688].rearrange("p (k t) -> p k t", k=8)
        msq = big_bf[:, 18688:20736].rearrange("p (k t) -> p k t", k=8)
        rmsnorm_fm(lambda k, h: mem_f[:, k, :], lambda k, h: R("memf"),
                   lambda k, h: mh[:, k, :], lambda k, h: R("mh", k), 8, GC_MEM, 1024.0, (0,), W=256,
                   sq=lambda k, h: msq[:, k, :], rsq=lambda k, h: R("msq", k))
        for hh in range(4):
            banks = [psA.next(), psA.next()]
            tsq = [TB(), TB()]
            tcp = [TF(), TF()]
            for cc in range(2):
                c16 = hh * 2 + cc

                def mmf(e, c16=c16, bank=banks[cc]):
                    ins = None
                    for k in range(8):
                        ins = e.matmul(pb[bank][:, 0:256], lhsT=wsel("AB", k, c16 * 128, 128), rhs=mh[:, k, :],
                                       start=(k == 0), stop=(k == 7))
                    return ins
                S.add("pe", mmf, reads=wres("AB") + [R("mh", k) for k in range(8)], writes=[RP(banks[cc])])
                S.add("act", lambda e, cc=cc, bank=banks[cc], t=tsq[cc][0]: e.activation(out=t[:, 0:256], in_=pb[bank][:, 0:256],
                                                                                        func=AF.Square),
                      reads=[RP(banks[cc])], writes=[tsq[cc][1]])
                S.add("dve", lambda e, cc=cc, bank=banks[cc], t=tcp[cc][0]: e.tensor_copy(out=t[:, 0:256], in_=pb[bank][:, 0:256]),
                      reads=[RP(banks[cc])], writes=[tcp[cc][1]])
            bn = psN.next()

            def mmn(e, bn=bn, a=tsq[0][0], b=tsq[1][0]):
                e.matmul(pb[bn][:, 0:256], lhsT=ones_b, rhs=a[:, 0:256], start=True, stop=False)
                return e.matmul(pb[bn][:, 0:256], lhsT=ones_b, rhs=b[:, 0:256], start=False, stop=True)
            S.add("pe", mmn, reads=[tsq[0][1], tsq[1][1], R("cst")], writes=[RP(bn)])
            trs, rrs = rstd_from(bn, 256, 256.0)
            for cc in range(2):
                S.add("dve", lambda e, cc=cc, c16=hh * 2 + cc, t=tcp[cc][0], trs=trs: e.scalar_tensor_tensor(
                    out=KmT[:, c16, :], in0=t[:, 0:256], scalar=gcol(GC_CK + cc), in1=trs[:, 0:256],
                    op0=ALU.mult, op1=ALU.mult),
                    reads=[tcp[cc][1], rrs, R("gv")], writes=[R("KmT")])
        for mc in range(2):
            for nh in range(2):
                bank = psA.next()

                def mmf(e, mc=mc, nh=nh, bank=bank):
                    ins = None
                    for k in range(8):
                        ins = e.matmul(pb[bank][:], lhsT=mh[:, k, mc * 128:(mc + 1) * 128], rhs=wsel("M", k, nh * 512, 512),
                                       start=(k == 0), stop=(k == 7))
                    return ins
                S.add("pe", mmf, reads=wres("M") + [R("mh", k) for k in range(8)], writes=[RP(bank)])
                S.add("act", lambda e, mc=mc, nh=nh, bank=bank: e.activation(out=Vm[:, mc, nh * 512:(nh + 1) * 512],
                                                                            in_=pb[bank][:], func=AF.Copy),
                      reads=[RP(bank)], writes=[R("Vm")])
        S.barrier()
        if stop_after == "A2":
            return finish(nc, S, st)

        kaug = [big_bf[:, 16640 + i * 4096:16640 + (i + 1) * 4096] for i in range(2)]
        qaug = [big_bf[:, 24832 + i * 2048:24832 + (i + 1) * 2048] for i in range(2)]
        pT = [big_bf[:, 28928 + i * 1024:28928 + (i + 1) * 1024] for i in range(2)]
        pTrot = Rot([0, 1])
        dtile = Rot([0, 1])
        def load_head(h):
            ks = h % 2
            S.add("sp", lambda e: e.dma_start(out=kaug[ks][0:70, :], in_=Kaug[h]), writes=[R("kaug", ks)], dma=True)
            S.add("sp", lambda e: e.dma_start(out=qaug[ks][0:70, :], in_=Qaug[h]), writes=[R("qaug", ks)], dma=True)

        LOOK = 2
        pending = []

        def finalize(h, r, ba):
            t_n, r_n = TF()
            t_r, r_r = TF()
            t_o, r_o = TF()
            t_h, r_h = TB()
            t_lo, r_lo = TB()

            def st1():
                S.add("dve", lambda e: e.tensor_copy(out=t_n[0:64, :], in_=pb[ba][0:64, :]), reads=[RP(ba)], writes=[r_n])
                S.add("dve", lambda e: e.reciprocal(out=t_r[64:65, :], in_=pb[ba][64:65, :]), reads=[RP(ba)], writes=[r_r])
                S.add("dve", lambda e: e.tensor_copy(out=t_h[64:65, :], in_=t_r[64:65, :]), reads=[r_r], writes=[r_h])
                S.add("dve", lambda e: e.tensor_copy(out=t_o[64:65, :], in_=t_h[64:65, :]), reads=[r_h], writes=[r_o])
                S.add("dve", lambda e: e.tensor_tensor(out=t_r[64:65, :], in0=t_r[64:65, :], in1=t_o[64:65, :], op=ALU.subtract),
                      reads=[r_o, r_r], writes=[r_r])
                S.add("dve", lambda e: e.tensor_copy(out=t_lo[64:65, :], in_=t_r[64:65, :]), reads=[r_r], writes=[r_lo])

            def st2():
                bn = psN.next()

                def mmb(e):
                    e.matmul(pb[bn][0:64, :], lhsT=ones_b[64:65, 0:64], rhs=t_h[64:65, :], start=True, stop=False)
                    return e.matmul(pb[bn][0:64, :], lhsT=ones_b[64:65, 0:64], rhs=t_lo[64:65, :], start=False, stop=True)
                S.add("pe", mmb, reads=[r_h, r_lo, R("cst")], writes=[RP(bn)])
                S.add("dve", lambda e: e.tensor_tensor(out=t_o[0:64, :], in0=pb[bn][0:64, :], in1=t_n[0:64, :], op=ALU.mult),
                      reads=[RP(bn), r_n, r_o], writes=[r_o])
                S.add("sp", lambda e: e.dma_start(out=Attn[h * 64:(h + 1) * 64, r * 512:(r + 1) * 512], in_=t_o[0:64, :]),
                      reads=[r_o], writes=[], dma=True)
            return [st1, st2]

        load_head(0)
        for h in range(8):
            ks = h % 2
            if h + 1 < 8:
                load_head(h + 1)
            for r in range(4):
                ba = psC.next()
                blocks = []
                for i in range(4):
                    blocks.append((16 + 4 * r + i, 0, "flag"))
                for j in range(r):
                    for i in range(4):
                        blocks.append((4 * j + i, 0, None))
                        blocks.append((16 + 4 * j + i, 0, None))
                for i in range(4):
                    blocks.append((4 * r + i, i * 128, "tri"))
                units = []
                i0 = 0
                while i0 < len(blocks):
                    if i0 + 1 < len(blocks) and blocks[i0][1] == 0 and blocks[i0 + 1][1] == 0 \
                            and blocks[i0][2] == blocks[i0 + 1][2] and blocks[i0][2] != "tri":
                        units.append([blocks[i0], blocks[i0 + 1]])
                        i0 += 2
                    else:
                        units.append([blocks[i0]])
                        i0 += 1
                nu = len(units)
                nb = len(blocks)
                info = []
                bcount = [0]
                for idx in range(nu + 1):
                    if idx < nu:
                        unit = units[idx]
                        dt_ = dtile.next()
                        pi = pTrot.next()
                        info.append(pi)

                        def mms(e, unit=unit, dt_=dt_, ks=ks, r=r):
                            ins = None
                            for ui, (kb, c0, mk) in enumerate(unit):
                                N = 512 - c0
                                ins = e.matmul(pbw[dt_][:, ui * 512:ui * 512 + N], lhsT=kaug[ks][0:70, kb * 128:(kb + 1) * 128],
                                               rhs=qaug[ks][0:70, r * 512 + c0:(r + 1) * 512], start=True, stop=(mk != "tri"))
                                if mk == "tri":
                                    ins = e.matmul(pbw[dt_][:, ui * 512:ui * 512 + 128], lhsT=ident, rhs=maskdiag, start=False,
                                                   stop=True, skip_group_check=True)
                            return ins
                        S.add("pe", mms, reads=[R("kaug", ks), R("qaug", ks), R("cst")], writes=[RP(2 * dt_), RP(2 * dt_ + 1)])
                        W_ = 1024 if len(unit) == 2 else 512 - unit[0][1]
                        if unit[0][2] == "flag":
                            S.add("act", lambda e, dt_=dt_, pi=pi, W_=W_, r=r: e.activation(
                                out=pT[pi][:, 0:W_], in_=pbw[dt_][:, 0:W_], func=AF.Exp, scale=0.125, bias=flagcol[:, r:r + 1]),
                                reads=[RP(2 * dt_), RP(2 * dt_ + 1), R("flagcol")], writes=[R("pT", pi)])
                        else:
                            S.add("act", lambda e, dt_=dt_, pi=pi, W_=W_: e.activation(out=pT[pi][:, 0:W_], in_=pbw[dt_][:, 0:W_],
                                                                                     func=AF.Exp, scale=0.125),
                                  reads=[RP(2 * dt_), RP(2 * dt_ + 1)], writes=[R("pT", pi)])
                    if idx >= 1:
                        unit = units[idx - 1]
                        pi = info[idx - 1]
                        first = bcount[0]

                        def mmpv(e, unit=unit, pi=pi, first=first, ba=ba, nb=nb, h=h):
                            ins = None
                            for ui, (kb, c0, mk) in enumerate(unit):
                                N = 512 - c0
                                bi = first + ui
                                ins = e.matmul(pb[ba][0:65, c0:512], lhsT=vaug[:, kb, h * 65:(h + 1) * 65],
                                               rhs=pT[pi][:, ui * 512:ui * 512 + N], start=(bi == 0), stop=(bi == nb - 1),
                                               skip_group_check=True)
                            return ins
                        bcount[0] += len(unit)
                        S.add("pe", mmpv, reads=[R("pT", pi), R("vaug")], writes=[RP(ba)])
                    if idx in (1, min(nu - 1, 5)) and pending:
                        pending.pop(0)()
                while pending:
                    pending.pop(0)()
                pending = finalize(h, r, ba)
        while pending:
            pending.pop(0)()
        load_wM(wo_d, 8192, 0)
        S.barrier()
        if stop_after == "B":
            return finish(nc, S, st)

        S.barrier()
        if stop_after == "M":
            return finish(nc, S, st)

        attnT = big_f[:, 0:4096].rearrange("p (m t) -> p m t", m=4)
        ca_o = actT

        def linear_residual(buf, src, rsrc):
            for d in range(8):
                for h in (0, 1):
                    by = psC.next()

                    def mmf(e, d=d, h=h, by=by):
                        ins = None
                        for k in range(8):
                            ins = e.matmul(pb[by][:], lhsT=wsel(buf, k, d * 128, 128), rhs=src(k, h), start=(k == 0), stop=(k == 7))
                        return ins
                    S.add("pe", mmf, reads=wres(buf) + [rsrc(k, h) for k in range(8)], writes=[RP(by)])
                    S.add("dve", lambda e, d=d, h=h, by=by: e.tensor_tensor(out=xg[:, d, hs(h)], in0=pb[by][:],
                                                                           in1=xg[:, d, hs(h)], op=ALU.add),
                          reads=[RP(by), R("xg", d, h)], writes=[R("xg", d, h)])

        out_ops = []

        def dump_x(t0):
            for k in range(8):
                out_ops.append(S.add("sp", lambda e, k=k, t0=t0: e.dma_start(out=outT[:, k, t0:t0 + 1024], in_=xg[:, k, :]),
                                     reads=[R("xg", k, 0), R("xg", k, 1)], writes=[], dma=True))
            S.fence("sp", out_ops)
            return finish(nc, S, st)

        def load_attn(G):
            for hh in (0, 1):
                for k in range(4):
                    S.add("sp" if k % 2 == 0 else "pool", lambda e, k=k, hh=hh, c0=G * 1024 + hh * 512: e.dma_start(
                        out=attnT[:, k, hs(hh)], in_=AttnV[:, k, c0:c0 + 512]), writes=[R("attnT", k, hh)], dma=True)

        def load_yg(G):
            for k in range(4):
                S.add("pool", lambda e, k=k, c0=G * 1024: e.dma_start(out=hT[:, 4 + k, :], in_=Yg[:, k, c0:c0 + 1024]),
                      writes=[R("hT", 4 + k, 0), R("hT", 4 + k, 1)], dma=True)

        for G in range(2):
            t0 = G * 1024
            if G == 0:
                load_attn(G)
                load_yg(G)
            for hh in (0, 1):
                S.add("pool", lambda e, c0=t0 + hh * 512, hh=hh: e.dma_start(out=xg[:, :, hs(hh)], in_=X1[:, :, c0:c0 + 512]),
                      reads=[R("dram_x1")], writes=[R("xg", k, hh) for k in range(8)], dma=True)
            rmsnorm_fm(lambda k, h: attnT[:, k, hs(h)], lambda k, h: R("attnT", k, h),
                       lambda k, h: hT[:, k, hs(h)], lambda k, h: R("hT", k, h), 4, GC_FOXO, 512.0, (0, 1),
                       sq=lambda k, h: actT[:, k, hs(h)], rsq=lambda k, h: R("act", k, h))
            prefetch_w(wcq_d, "AB")
            linear_residual("M", lambda k, h: hT[:, k, hs(h)], lambda k, h: R("hT", k, h))
            if stop_after == "C1":
                return dump_x(t0)
            norm_x_to_h(GC_CA)
            prefetch_w(wco_d, "M")

            def ca_stages(item, hh, h):
                c = {}
                pb0 = (item % 2) * 2

                def st0():
                    c["banks"] = [0, 1]
                    for cc in range(2):
                        c16 = hh * 2 + cc

                        def mmf(e, c16=c16, bank=c["banks"][cc]):
                            ins = None
                            for k in range(8):
                                ins = e.matmul(pb[bank][:], lhsT=wsel("AB", k, c16 * 128, 128), rhs=hT[:, k, hs(h)],
                                               start=(k == 0), stop=(k == 7))
                            return ins
                        S.add("pe", mmf, reads=wres("AB") + [R("hT", k, h) for k in range(8)], writes=[RP(c["banks"][cc])])

                def st1():
                    c["tsq"] = [TB(), TB()]
                    c["tcp"] = [TF(), TF()]
                    for cc in range(2):
                        S.add("act", lambda e, bank=c["banks"][cc], t=c["tsq"][cc][0]: e.activation(out=t[:], in_=pb[bank][:], func=AF.Square),
                              reads=[RP(c["banks"][cc])], writes=[c["tsq"][cc][1]])
                        S.add("dve", lambda e, bank=c["banks"][cc], t=c["tcp"][cc][0]: e.tensor_copy(out=t[:], in_=pb[bank][:]),
                              reads=[RP(c["banks"][cc])], writes=[c["tcp"][cc][1]])

                def st2():
                    c["bn"] = 6

                    def mmn(e, bn=c["bn"], a=c["tsq"][0][0], b=c["tsq"][1][0]):
                        e.matmul(pb[bn][:], lhsT=ones_b, rhs=a[:], start=True, stop=False)
                        return e.matmul(pb[bn][:], lhsT=ones_b, rhs=b[:], start=False, stop=True)
                    S.add("pe", mmn, reads=[c["tsq"][0][1], c["tsq"][1][1], R("cst")], writes=[RP(c["bn"])])

                def st3():
                    trs, rrs = rstd_from(c["bn"], 512, 256.0)
                    c["tqn"] = [TB(), TB()]
                    for cc in range(2):
                        S.add("dve", lambda e, cc=cc, t=c["tcp"][cc][0], o=c["tqn"][cc][0], trs=trs: e.scalar_tensor_tensor(
                            out=o[:], in0=t[:], scalar=gcol(GC_CQ + cc), in1=trs[:], op0=ALU.mult, op1=ALU.mult),
                            reads=[c["tcp"][cc][1], rrs, R("gv")], writes=[c["tqn"][cc][1]])

                def st4():
                    for mc in range(2):
                        bs_ = 2 + mc

                        def mms(e, mc=mc, bs_=bs_, a=c["tqn"][0][0], b=c["tqn"][1][0]):
                            e.matmul(pb[bs_][:], lhsT=KmT[:, hh * 2, mc * 128:(mc + 1) * 128], rhs=a[:], start=True, stop=False)
                            return e.matmul(pb[bs_][:], lhsT=KmT[:, hh * 2 + 1, mc * 128:(mc + 1) * 128], rhs=b[:],
                                            start=False, stop=True)
                        S.add("pe", mms, reads=[R("KmT"), c["tqn"][0][1], c["tqn"][1][1]], writes=[RP(bs_)])
                        S.add("act", lambda e, mc=mc, bs_=bs_: e.activation(out=actT[:, pb0 + mc, 0:512], in_=pb[bs_][:], func=AF.Exp,
                                                                           scale=1.0 / 16),
                              reads=[RP(bs_)], writes=[R("act", pb0 + mc, 0)])

                def st5():
                    c["bd"] = 7

                    def mmd(e, bn=c["bd"]):
                        e.matmul(pb[bn][:], lhsT=ones_b, rhs=actT[:, pb0, 0:512], start=True, stop=False)
                        return e.matmul(pb[bn][:], lhsT=ones_b, rhs=actT[:, pb0 + 1, 0:512], start=False, stop=True)
                    S.add("pe", mmd, reads=[R("act", pb0, 0), R("act", pb0 + 1, 0), R("cst")], writes=[RP(c["bd"])])
                    c["bo"] = []
                    for cc in range(2):
                        bo = 4 + cc
                        c["bo"].append(bo)
                        c16 = hh * 2 + cc

                        def mmo(e, bo=bo, c16=c16):
                            e.matmul(pb[bo][:], lhsT=Vm[:, 0, c16 * 128:(c16 + 1) * 128], rhs=actT[:, pb0, 0:512], start=True, stop=False)
                            return e.matmul(pb[bo][:], lhsT=Vm[:, 1, c16 * 128:(c16 + 1) * 128], rhs=actT[:, pb0 + 1, 0:512],
                                            start=False, stop=True)
                        S.add("pe", mmo, reads=[R("Vm"), R("act", pb0, 0), R("act", pb0 + 1, 0)], writes=[RP(bo)])

                def st6():
                    t_rd, r_rd = TF()
                    S.add("act", lambda e, bn=c["bd"]: e.activation(out=t_rd[:], in_=pb[bn][:], func=AF.Ln), reads=[RP(c["bd"])], writes=[r_rd])
                    S.add("act", lambda e: e.activation(out=t_rd[:], in_=t_rd[:], func=AF.Exp, scale=-1.0), reads=[r_rd], writes=[r_rd])
                    for cc in range(2):
                        c16 = hh * 2 + cc
                        S.add("dve", lambda e, bo=c["bo"][cc], c16=c16: e.tensor_tensor(
                            out=actT[:, 10 + c16, hs(h)], in0=pb[bo][:], in1=t_rd[:], op=ALU.mult),
                            reads=[RP(c["bo"][cc]), r_rd], writes=[R("act", 10 + c16, h)])
                return [st0, st1, st2, st3, st4, st5, st6]

            pipeline([ca_stages(i, i // 2, i % 2) for i in range(8)])
            if stop_after == "C2a":
                return dump_x(t0)
            linear_residual("M", lambda k, h: actT[:, 10 + k, hs(h)], lambda k, h: R("act", 10 + k, h))
            if stop_after == "C2":
                return dump_x(t0)
            ffn(w2a, w2b, GC_FFN2)
            if G + 1 < 2:
                load_attn(G + 1)
                load_yg(G + 1)
                prefetch_w(wo_d, "M")
            for k in range(8):
                out_ops.append(S.add("sp" if k % 2 == 0 else "pool",
                                     lambda e, k=k, t0=t0: e.dma_start(out=outT[:, k, t0:t0 + 1024], in_=xg[:, k, :]),
                                     reads=[R("xg", k, 0), R("xg", k, 1)], writes=[], dma=True))
        S.fence("sp", out_ops)
        return finish(nc, S, st)


def finish(nc, S, st):
    S.barrier()
    S.emit()
    return nc


def _pk(w, ncols_off, ncols):
    blk = w[:, ncols_off:ncols_off + ncols]
    return np.ascontiguousarray(blk.reshape(8, 128, ncols).transpose(1, 0, 2)).reshape(128, 8 * ncols)


def _ffn_layout(w_in, w_out):
    a = w_in.reshape(8, 128, 2, NF, 128)
    a = np.ascontiguousarray(a.transpose(3, 1, 2, 0, 4)).reshape(NF, 128, 2048)
    b = w_out.reshape(NF, 128, 8, 128)
    b = np.ascontiguousarray(b.transpose(2, 1, 0, 3)).reshape(8, 128, NF * 128)
    return a, b


def _run_order(p):
    own = OWN_RUNS[p]
    oth = OWN_RUNS[1 - p]
    return own + oth


def make_in_maps(inp):
    f32 = lambda a: np.ascontiguousarray(np.asarray(a, dtype=np.float32))
    x = f32(inp["x"])
    mem = f32(inp["mem"])
    w_in = f32(inp["w_in"])[0]
    shared = {}
    shared["w1a"], shared["w1b"] = _ffn_layout(f32(inp["w_ffn1_in"])[0], f32(inp["w_ffn1_out"])[0])
    shared["w2a"], shared["w2b"] = _ffn_layout(f32(inp["w_ffn2_in"])[0], f32(inp["w_ffn2_out"])[0])
    shared["wq"] = _pk(w_in, 0, 512)
    shared["wk"] = _pk(w_in, 512, 512)
    shared["wv"] = _pk(w_in, 1024, 512)
    shared["wf"] = _pk(w_in, 1536, 8)
    shared["wu"] = _pk(w_in, 1544, 512)
    shared["wvg"] = _pk(w_in, 1544 + 512, 512)
    shared["wo"] = _pk(f32(inp["w_out"])[0], 0, 1024)
    shared["wcq"] = _pk(f32(inp["w_cq"])[0], 0, 1024)
    wckv = f32(inp["w_ckv"])[0]
    shared["wck"] = _pk(wckv, 0, 1024)
    shared["wcv"] = _pk(wckv, 1024, 1024)
    shared["wco"] = _pk(f32(inp["w_co"])[0], 0, 1024)
    gvec = np.zeros((128, 64), np.float32)

    def colset(c0, v, n):
        gvec[:, c0:c0 + n] = v.reshape(n, 128).T
    colset(GC_FFN1, f32(inp["g_ffn1"])[0], 8)
    colset(GC_MIX, f32(inp["g_mix"])[0], 8)
    colset(GC_CA, f32(inp["g_ca"])[0], 8)
    colset(GC_MEM, f32(inp["g_mem"])[0], 8)
    colset(GC_FFN2, f32(inp["g_ffn2"])[0], 8)
    colset(GC_FOXO, f32(inp["g_fox_o"])[0], 4)
    colset(GC_GMLPO, f32(inp["g_gmlp_o"])[0], 4)
    gvec[:, GC_Q] = np.tile(f32(inp["g_q"])[0], 2)
    gvec[:, GC_K] = np.tile(f32(inp["g_k"])[0], 2)
    colset(GC_CQ, f32(inp["g_cq"])[0], 2)
    colset(GC_CK, f32(inp["g_ck"])[0], 2)
    gvec[0:8, GC_BF] = f32(inp["b_f"])[0]
    shared["gvec"] = gvec
    shared["gsgu"] = np.ascontiguousarray(np.broadcast_to(f32(inp["g_sgu"])[0][None, :], (128, 512)))
    ws = f32(inp["w_s"])[0]
    shared["wsT"] = np.ascontiguousarray(ws.transpose(2, 0, 1)).reshape(128, 1024)
    s_idx = np.arange(128)
    shared["tril"] = (s_idx[:, None] <= s_idx[None, :]).astype(np.float32)
    bs = f32(inp["b_s"])[0]
    shared["bsb"] = np.ascontiguousarray(bs.reshape(4, 2, 128).transpose(1, 0, 2)[:, None].repeat(64, 1)
                                         .reshape(128, 4, 128)).reshape(128, 512)
    cst = np.zeros((128, 4, 128), np.float32)
    cst[:, 0, :] = np.eye(128, dtype=np.float32)
    cst[:, 1, :] = 1.0
    cst[0:64, 2, 0:64] = 1.0
    cst[64:128, 2, 64:128] = 1.0
    cst[:, 3, :] = np.where(s_idx[:, None] > s_idx[None, :], NEG, 0.0)
    shared["cst"] = cst.reshape(128, 512)
    in_maps = []
    for c in range(8):
        b, p = c // 2, c % 2
        order = _run_order(p)
        xT = x[b].T
        xTp = np.ascontiguousarray(xT.reshape(1024, 8, 512)[:, order, :]).reshape(1024, 4096)
        m = dict(shared)
        m["xT"] = xTp
        m["memT"] = np.ascontiguousarray(mem[b].T)
        mord = np.zeros((8, 8), np.float32)
        for r in range(8):
            for r2 in range(8):
                mord[r, r2] = 1.0 if order[r2] < order[r] else 0.0
        m64 = np.zeros((8, 8, 8, 8), np.float32)
        for hh in range(8):
            m64[:, hh, :, hh] = mord.T
        m["mord"] = m64.reshape(64, 64)
        nf = np.zeros((128, 4), np.float32)
        own, oth = OWN_RUNS[p], OWN_RUNS[1 - p]
        for r in range(4):
            if oth[r] > own[r]:
                nf[:, r] = NEG / 8.0
        m["flagcol"] = nf
        in_maps.append(m)
    return in_maps


_CACHE = {}


def kernel(**inputs):
    if "nc" not in _CACHE:
        _CACHE["nc"] = build_program()
    nc = _CACHE["nc"]
    in_maps = make_in_maps(inputs)
    res = run_bass_kernel_spmd(nc, in_maps, core_ids=list(range(8)))
    out = np.zeros((4, 4096, 1024), np.float32)
    for c in range(8):
        b, p = c // 2, c % 2
        oT = np.asarray(res.results[c]["outT"]).reshape(1024, 4, 512)
        for i, r in enumerate(OWN_RUNS[p]):
            out[b, r * 512:(r + 1) * 512, :] = oT[:, i, :].T
    return out
```
